# Optimizing a Trainium2 kernel written in Bass

```python
import jax, jax.numpy as jnp
from jax import lax
import numpy as np

D_MODEL = 1024
BATCH = 4
SEQ = 4096
DEPTH = 2
DEC_BATCH = 32
DEC_SEQ = 4
PAST_LEN = 8192
PAGE_SIZE = 128

N_HEADS = 16
N_KV_HEADS = 4
HEAD_DIM = 64
D_ATTN = N_HEADS * HEAD_DIM
IDX_HEADS = 8
IDX_DIM = 64
INDEX_TOPK = 256
ATTN_Q_BLOCK = 128
ROPE_THETA = 10000.0
POOL_WINDOWS = (2, 4, 8, 16)
D_POOL = D_MODEL
N_POOL_GROUPS = len(POOL_WINDOWS)
POOL_GROUP = D_POOL // N_POOL_GROUPS
POOL_CTX = max(POOL_WINDOWS) - 1
SSD_EXPAND = 2
D_SSD = SSD_EXPAND * D_MODEL
SSD_HEAD_DIM = 64
SSD_HEADS = D_SSD // SSD_HEAD_DIM
SSD_GROUPS = 4
SSD_STATE = 128
SSD_CONV = 4
SSD_CHUNK = 128
D_SSD_CONV = D_SSD + 2 * SSD_GROUPS * SSD_STATE
D_FF = 2816
FFN_CONV = 3
N_BRANCH = 3
EPS = 1e-6

_SPLITS = (D_ATTN, N_KV_HEADS * HEAD_DIM, N_KV_HEADS * HEAD_DIM, IDX_HEADS * IDX_DIM, IDX_DIM,
           IDX_HEADS, D_POOL, D_SSD, D_SSD_CONV, SSD_HEADS, N_BRANCH * D_MODEL)
D_IN = sum(_SPLITS)
_OFFSETS = tuple(int(v) for v in np.cumsum(_SPLITS)[:-1])

kernel_name = 'hybrid_dsa_pool_ssd_convffn_adaln_step'


def _rmsnorm(x, g):
    xf = x.astype(jnp.float32)
    y = xf * lax.rsqrt(jnp.mean(xf * xf, axis=-1, keepdims=True) + EPS)
    return (y * g.astype(jnp.float32)).astype(x.dtype)


def _rope(x, pos):
    half = x.shape[-1] // 2
    freqs = ROPE_THETA ** (-jnp.arange(half, dtype=jnp.float32) / half)
    ang = pos.astype(jnp.float32)[:, None] * freqs[None, :]
    cos = jnp.cos(ang)[:, None, :]
    sin = jnp.sin(ang)[:, None, :]
    xf = x.astype(jnp.float32)
    x1, x2 = xf[..., :half], xf[..., half:]
    return jnp.concatenate([x1 * cos - x2 * sin, x1 * sin + x2 * cos], axis=-1).astype(x.dtype)


def _causal_dwconv(u, prev, w, b):
    width = w.shape[0]
    T = u.shape[1]
    ext = jnp.concatenate([prev.astype(u.dtype), u], axis=1)
    out = b + ext[:, 0:T] * w[0]
    for j in range(1, width):
        out = out + ext[:, j:j + T] * w[j]
    return out, ext[:, T:]


def _gather_pages(pool, table):
    g = pool[table]
    return g.reshape((table.shape[0], table.shape[1] * pool.shape[1]) + pool.shape[2:])


def _sparse_attention(q, k_all, v_all, qi, ki_all, wi, qpos, topk):
    B, T = q.shape[0], q.shape[1]
    S = k_all.shape[1]
    blk = ATTN_Q_BLOCK if T % ATTN_Q_BLOCK == 0 else T
    nb = T // blk
    kpos = jnp.arange(S)
    ki_f = ki_all.astype(jnp.float32)

    def to_blocks(a):
        return jnp.moveaxis(a.reshape((B, nb, blk) + a.shape[2:]), 1, 0)

    def block(args):
        q_b, qi_b, wi_b, p_b = args
        rel = jax.nn.relu(jnp.einsum('bqhd,bsd->bqsh', qi_b.astype(jnp.float32), ki_f) * IDX_DIM ** -0.5)
        score = jnp.einsum('bqsh,bqh->bqs', rel, wi_b.astype(jnp.float32))
        visible = kpos[None, None, :] <= p_b[None, :, None]
        score = jnp.where(visible, score, -jnp.inf)
        _, idx = lax.top_k(score, topk)
        valid = idx <= p_b[None, :, None]
        ks = jax.vmap(lambda kk, ii: kk[ii])(k_all, idx).astype(jnp.float32)
        vs = jax.vmap(lambda vv, ii: vv[ii])(v_all, idx).astype(jnp.float32)
        qg = q_b.reshape(B, blk, N_KV_HEADS, N_HEADS // N_KV_HEADS, HEAD_DIM).astype(jnp.float32)
        s = jnp.einsum('bqgrd,bqkgd->bqgrk', qg, ks) * HEAD_DIM ** -0.5
        s = jnp.where(valid[:, :, None, None, :], s, -jnp.inf)
        p = jax.nn.softmax(s, axis=-1)
        o = jnp.einsum('bqgrk,bqkgd->bqgrd', p, vs)
        return o.reshape(B, blk, D_ATTN).astype(q.dtype)

    out = lax.map(block, (to_blocks(q), to_blocks(qi), to_blocks(wi), qpos.reshape(nb, blk)))
    return jnp.moveaxis(out, 0, 1).reshape(B, T, D_ATTN)


def _pool_mix(u, prev, pos, w_grp, scale):
    B, T, C = u.shape
    ext = jnp.concatenate([prev.astype(u.dtype), u], axis=1)
    cs = jnp.cumsum(ext.astype(jnp.float32), axis=1)
    cs = jnp.concatenate([jnp.zeros((B, 1, C), jnp.float32), cs], axis=1)
    hi = cs[:, POOL_CTX + 1:]
    groups = []
    for gi, w in enumerate(POOL_WINDOWS):
        sl = slice(gi * POOL_GROUP, (gi + 1) * POOL_GROUP)
        lo = cs[:, POOL_CTX + 1 - w:POOL_CTX + 1 - w + T, sl]
        cnt = jnp.minimum(w, pos + 1).astype(jnp.float32)[None, :, None]
        groups.append((hi[..., sl] - lo) / cnt)
    pooled = jnp.concatenate(groups, axis=-1) - u.astype(jnp.float32)
    mixed = jnp.einsum('btgc,gcd->btgd', pooled.reshape(B, T, N_POOL_GROUPS, POOL_GROUP),
                       w_grp.astype(jnp.float32)).reshape(B, T, C) * scale.astype(jnp.float32)
    return mixed.astype(u.dtype), ext[:, T:]


def _ssd_scan(x, dt, A, Bm, Cm, h0):
    b, T, H, P = x.shape
    G, N = Bm.shape[2], Bm.shape[3]
    R = H // G
    Q = min(SSD_CHUNK, T)
    pad = (-T) % Q
    if pad:
        x = jnp.pad(x, ((0, 0), (0, pad), (0, 0), (0, 0)))
        dt = jnp.pad(dt, ((0, 0), (0, pad), (0, 0)))
        Bm = jnp.pad(Bm, ((0, 0), (0, pad), (0, 0), (0, 0)))
        Cm = jnp.pad(Cm, ((0, 0), (0, pad), (0, 0), (0, 0)))
    nc = (T + pad) // Q
    x = x.reshape(b, nc, Q, G, R, P)
    dt = dt.reshape(b, nc, Q, G, R)
    Bm = Bm.reshape(b, nc, Q, G, N)
    Cm = Cm.reshape(b, nc, Q, G, N)
    cs = jnp.cumsum(dt * A.reshape(G, R), axis=2)
    cs_t = jnp.moveaxis(cs, 2, -1)
    seg = cs_t[..., :, None] - cs_t[..., None, :]
    causal = jnp.tril(jnp.ones((Q, Q), dtype=bool))
    Lmat = jnp.where(causal, jnp.exp(jnp.where(causal, seg, 0.0)), 0.0)
    CB = jnp.einsum('bclgn,bcsgn->bcgls', Cm, Bm)
    xdt = x * dt[..., None]
    y_diag = jnp.einsum('bcgrls,bcsgrp->bclgrp', CB[:, :, :, None] * Lmat, xdt)
    decay = jnp.exp(cs[:, :, -1:] - cs)
    states = jnp.einsum('bclgn,bclgrp->bcgrpn', Bm, xdt * decay[..., None])
    chunk_decay = jnp.exp(cs[:, :, -1])

    def step(h, inp):
        s, d = inp
        return h * d[..., None, None] + s, h

    hT, prev = lax.scan(step, h0.reshape(b, G, R, P, N),
                        (jnp.moveaxis(states, 1, 0), jnp.moveaxis(chunk_decay, 1, 0)))
    prev = jnp.moveaxis(prev, 0, 1)
    y_off = jnp.einsum('bclgn,bcgrpn->bclgrp', Cm, prev) * jnp.exp(cs)[..., None]
    y = (y_diag + y_off).reshape(b, nc * Q, H, P)[:, :T]
    return y, hT.reshape(b, H, P, N)


def _ssd_mix(z, xbc, dt_raw, conv_prev, h0, conv_w, conv_b, dt_bias, a_log, d_skip, norm_w):
    B, T, _ = z.shape
    xbc, conv_state = _causal_dwconv(xbc, conv_prev, conv_w, conv_b)
    xbc = jax.nn.silu(xbc.astype(jnp.float32))
    xs = xbc[..., :D_SSD].reshape(B, T, SSD_HEADS, SSD_HEAD_DIM)
    Bm = xbc[..., D_SSD:D_SSD + SSD_GROUPS * SSD_STATE].reshape(B, T, SSD_GROUPS, SSD_STATE)
    Cm = xbc[..., D_SSD + SSD_GROUPS * SSD_STATE:].reshape(B, T, SSD_GROUPS, SSD_STATE)
    dt = jax.nn.softplus(dt_raw.astype(jnp.float32) + dt_bias.astype(jnp.float32))
    A = -jnp.exp(a_log.astype(jnp.float32))
    y, hT = _ssd_scan(xs, dt, A, Bm, Cm, h0.astype(jnp.float32))
    y = y + d_skip.astype(jnp.float32)[:, None] * xs
    y = y.reshape(B, T, D_SSD) * jax.nn.silu(z.astype(jnp.float32))
    yg = y.reshape(B, T, SSD_GROUPS, D_SSD // SSD_GROUPS)
    yg = yg * lax.rsqrt(jnp.mean(yg * yg, axis=-1, keepdims=True) + EPS)
    y = yg.reshape(B, T, D_SSD) * norm_w.astype(jnp.float32)
    return y.astype(z.dtype), conv_state, hT.astype(h0.dtype)


def _trunk_layer(x, c, pos, kv_past, pool_prev, sconv_prev, ssd_h0, fconv_prev, lw):
    B, T, _ = x.shape
    mod = (jax.nn.silu(c) @ lw['w_ada'] + lw['b_ada'])[:, None, :]
    sh1, sc1, g1, sh2, sc2, g2 = jnp.split(mod, 6, axis=-1)

    h = _rmsnorm(x, lw['norm1']) * (1 + sc1) + sh1
    proj = h @ lw['w_in']
    q, k, v, qi, ki, wi, pool_u, z, xbc, dt_raw, gates = jnp.split(proj, _OFFSETS, axis=-1)

    q = _rope(_rmsnorm(q.reshape(B, T, N_HEADS, HEAD_DIM), lw['q_norm']), pos)
    k = _rope(_rmsnorm(k.reshape(B, T, N_KV_HEADS, HEAD_DIM), lw['k_norm']), pos)
    v = v.reshape(B, T, N_KV_HEADS, HEAD_DIM)
    qi = _rope(qi.reshape(B, T, IDX_HEADS, IDX_DIM), pos)
    ki = _rope(ki.reshape(B, T, 1, IDX_DIM), pos)[:, :, 0]
    wi = wi * IDX_HEADS ** -0.5
    if kv_past is None:
        k_all, v_all, ki_all = k, v, ki
    else:
        kp, vp, kip = kv_past
        k_all = jnp.concatenate([kp.astype(k.dtype), k], axis=1)
        v_all = jnp.concatenate([vp.astype(v.dtype), v], axis=1)
        ki_all = jnp.concatenate([kip.astype(ki.dtype), ki], axis=1)
    topk = min(INDEX_TOPK, k_all.shape[1] // 4)
    o_attn = _sparse_attention(q, k_all, v_all, qi, ki_all, wi, pos, topk)

    o_pool, pool_state = _pool_mix(pool_u, pool_prev, pos, lw['pool_w'], lw['pool_scale'])

    o_ssd, sconv_state, ssd_state = _ssd_mix(z, xbc, dt_raw, sconv_prev, ssd_h0, lw['ssd_conv_w'],
                                             lw['ssd_conv_b'], lw['ssd_dt_bias'], lw['ssd_a_log'],
                                             lw['ssd_d'], lw['ssd_norm'])

    gt = jax.nn.sigmoid(gates.astype(jnp.float32)).reshape(B, T, N_BRANCH, D_MODEL)
    merged = (gt[:, :, 0] * (o_attn @ lw['w_branch_attn'])
              + gt[:, :, 1] * (o_pool @ lw['w_branch_pool'])
              + gt[:, :, 2] * (o_ssd @ lw['w_branch_ssd']))
    x = x + g1 * (merged.astype(x.dtype) @ lw['w_out'])

    h2 = _rmsnorm(x, lw['norm2']) * (1 + sc2) + sh2
    u = h2 @ lw['ffn_up']
    uc, fconv_state = _causal_dwconv(u, fconv_prev, lw['ffn_conv_w'], lw['ffn_conv_b'])
    a, gg = jnp.split(uc, 2, axis=-1)
    x = x + g2 * ((jax.nn.silu(gg) * a) @ lw['ffn_down'])
    return x, (k, v, ki, pool_state, sconv_state, ssd_state, fconv_state)


def setup_inputs(seed: int = 0) -> dict:
    key = jax.random.key(seed)
    ks = jax.random.split(key, 40)
    f32 = jnp.float32

    def nrm(k, shape, scale):
        return jax.random.normal(k, shape, f32) * scale

    n_pages = PAST_LEN // PAGE_SIZE
    n_used = DEC_BATCH * n_pages
    n_pool = n_used + n_used // 4
    page_table = jax.random.permutation(ks[0], n_pool)[:n_used].reshape(DEC_BATCH, n_pages).astype(jnp.int32)
    dt0 = jnp.exp(jax.random.uniform(ks[1], (DEPTH, SSD_HEADS), f32, np.log(1e-3), np.log(1e-1)))
    return {
        'x_prompt': nrm(ks[2], (BATCH, SEQ, D_MODEL), 1.0),
        'x_sample': nrm(ks[3], (DEC_BATCH, DEC_SEQ, D_MODEL), 1.0),
        'c_prompt': nrm(ks[4], (BATCH, D_MODEL), 1.0),
        'c_sample': nrm(ks[5], (DEC_BATCH, D_MODEL), 1.0),
        'cache_k': nrm(ks[6], (DEPTH, n_pool, PAGE_SIZE, N_KV_HEADS, HEAD_DIM), 1.0),
        'cache_v': nrm(ks[7], (DEPTH, n_pool, PAGE_SIZE, N_KV_HEADS, HEAD_DIM), 1.0),
        'cache_kidx': nrm(ks[8], (DEPTH, n_pool, PAGE_SIZE, IDX_DIM), 1.0),
        'page_table': page_table,
        'state_pool': nrm(ks[9], (DEPTH, DEC_BATCH, POOL_CTX, D_POOL), 1.0),
        'state_ssd_conv': nrm(ks[10], (DEPTH, DEC_BATCH, SSD_CONV - 1, D_SSD_CONV), 1.0),
        'state_ssd': nrm(ks[11], (DEPTH, DEC_BATCH, SSD_HEADS, SSD_HEAD_DIM, SSD_STATE), 0.1),
        'state_ffn_conv': nrm(ks[12], (DEPTH, DEC_BATCH, FFN_CONV - 1, 2 * D_FF), 1.0),
        'w_ada': nrm(ks[13], (DEPTH, D_MODEL, 6 * D_MODEL), 0.5 * D_MODEL ** -0.5),
        'b_ada': nrm(ks[14], (DEPTH, 6 * D_MODEL), 0.02),
        'norm1': 1.0 + nrm(ks[15], (DEPTH, D_MODEL), 0.05),
        'norm2': 1.0 + nrm(ks[16], (DEPTH, D_MODEL), 0.05),
        'w_in': nrm(ks[17], (DEPTH, D_MODEL, D_IN), D_MODEL ** -0.5),
        'q_norm': 1.0 + nrm(ks[18], (DEPTH, HEAD_DIM), 0.05),
        'k_norm': 1.0 + nrm(ks[19], (DEPTH, HEAD_DIM), 0.05),
        'pool_w': nrm(ks[20], (DEPTH, N_POOL_GROUPS, POOL_GROUP, POOL_GROUP), POOL_GROUP ** -0.5),
        'pool_scale': 1.0 + nrm(ks[21], (DEPTH, D_POOL), 0.1),
        'ssd_conv_w': nrm(ks[22], (DEPTH, SSD_CONV, D_SSD_CONV), SSD_CONV ** -0.5),
        'ssd_conv_b': nrm(ks[23], (DEPTH, D_SSD_CONV), 0.02),
        'ssd_dt_bias': dt0 + jnp.log(-jnp.expm1(-dt0)),
        'ssd_a_log': jnp.log(jax.random.uniform(ks[24], (DEPTH, SSD_HEADS), f32, 1.0, 16.0)),
        'ssd_d': 1.0 + nrm(ks[25], (DEPTH, SSD_HEADS), 0.1),
        'ssd_norm': 1.0 + nrm(ks[26], (DEPTH, D_SSD), 0.05),
        'w_branch_attn': nrm(ks[27], (DEPTH, D_ATTN, D_MODEL), D_ATTN ** -0.5),
        'w_branch_pool': nrm(ks[28], (DEPTH, D_POOL, D_MODEL), D_POOL ** -0.5),
        'w_branch_ssd': nrm(ks[29], (DEPTH, D_SSD, D_MODEL), D_SSD ** -0.5),
        'w_out': nrm(ks[30], (DEPTH, D_MODEL, D_MODEL), D_MODEL ** -0.5),
        'ffn_up': nrm(ks[31], (DEPTH, D_MODEL, 2 * D_FF), D_MODEL ** -0.5),
        'ffn_conv_w': nrm(ks[32], (DEPTH, FFN_CONV, 2 * D_FF), FFN_CONV ** -0.5),
        'ffn_conv_b': nrm(ks[33], (DEPTH, 2 * D_FF), 0.02),
        'ffn_down': nrm(ks[34], (DEPTH, D_FF, D_MODEL), D_FF ** -0.5),
    }


def reference(x_prompt, x_sample, c_prompt, c_sample, cache_k, cache_v, cache_kidx, page_table,
              state_pool, state_ssd_conv, state_ssd, state_ffn_conv, w_ada, b_ada, norm1, norm2, w_in,
              q_norm, k_norm, pool_w, pool_scale, ssd_conv_w, ssd_conv_b, ssd_dt_bias, ssd_a_log, ssd_d,
              ssd_norm, w_branch_attn, w_branch_pool, w_branch_ssd, w_out, ffn_up, ffn_conv_w, ffn_conv_b,
              ffn_down):
    bp, tp = x_prompt.shape[0], x_prompt.shape[1]
    ts = x_sample.shape[1]
    past_len = page_table.shape[1] * cache_k.shape[2]
    pos_p = jnp.arange(tp)
    pos_s = past_len + jnp.arange(ts)
    dtp = x_prompt.dtype
    xp, xs = x_prompt, x_sample
    new_p = [[] for _ in range(7)]
    new_s = [[] for _ in range(7)]
    for l in range(DEPTH):
        lw = dict(w_ada=w_ada[l], b_ada=b_ada[l], norm1=norm1[l], norm2=norm2[l], w_in=w_in[l],
                  q_norm=q_norm[l], k_norm=k_norm[l], pool_w=pool_w[l], pool_scale=pool_scale[l],
                  ssd_conv_w=ssd_conv_w[l], ssd_conv_b=ssd_conv_b[l], ssd_dt_bias=ssd_dt_bias[l],
                  ssd_a_log=ssd_a_log[l], ssd_d=ssd_d[l], ssd_norm=ssd_norm[l],
                  w_branch_attn=w_branch_attn[l], w_branch_pool=w_branch_pool[l],
                  w_branch_ssd=w_branch_ssd[l], w_out=w_out[l], ffn_up=ffn_up[l],
                  ffn_conv_w=ffn_conv_w[l], ffn_conv_b=ffn_conv_b[l], ffn_down=ffn_down[l])
        xp, st_p = _trunk_layer(
            xp, c_prompt, pos_p, None,
            jnp.zeros((bp, POOL_CTX, D_POOL), dtp),
            jnp.zeros((bp, SSD_CONV - 1, D_SSD_CONV), dtp),
            jnp.zeros((bp, SSD_HEADS, SSD_HEAD_DIM, SSD_STATE), dtp),
            jnp.zeros((bp, FFN_CONV - 1, 2 * D_FF), dtp), lw)
        kv_past = (_gather_pages(cache_k[l], page_table), _gather_pages(cache_v[l], page_table),
                   _gather_pages(cache_kidx[l], page_table))
        xs, st_s = _trunk_layer(xs, c_sample, pos_s, kv_past, state_pool[l], state_ssd_conv[l],
                                state_ssd[l], state_ffn_conv[l], lw)
        for j in range(7):
            new_p[j].append(st_p[j])
            new_s[j].append(st_s[j])
    nk_p, nv_p, nki_p, npool_p, nsconv_p, nssd_p, nfconv_p = [jnp.stack(s) for s in new_p]
    nk_s, nv_s, nki_s, npool_s, nsconv_s, nssd_s, nfconv_s = [jnp.stack(s) for s in new_s]
    return (xp, xs, nk_p, nv_p, nki_p, npool_p, nsconv_p, nssd_p, nfconv_p,
            nk_s, nv_s, nki_s, npool_s, nsconv_s, nssd_s, nfconv_s)
```

```python
import numpy as np
import concourse.bass as bass
import concourse.mybir as mybir
from concourse.bass_utils import run_bass_kernel_spmd

F32 = mybir.dt.float32
BF16 = mybir.dt.bfloat16
I32 = mybir.dt.int32
AF = mybir.ActivationFunctionType
ALU = mybir.AluOpType
AX = mybir.AxisListType

D = 1024
L = 2
NH, NKV, HD = 16, 4, 64
IH, ID = 8, 64
DSSD, SH, SP, SG, SN = 2048, 32, 64, 4, 128
DCONV = 3072
DFF = 2816
DIN = 11368
O_Q, O_K, O_V, O_QI, O_KI, O_WI, O_PU, O_Z, O_XBC, O_DT, O_G = 0, 1024, 1280, 1536, 2048, 2112, 2120, 3144, 5192, 8264, 8296
EPS = 1e-6
NEG = -1.0e30
NBIS = 20


class Op:
    __slots__ = ("eng", "fn", "deps", "dma", "dsem", "dval", "dprev", "inc", "idx", "cc")


class Trk:
    __slots__ = ("w", "rd", "prev")

    def __init__(self, prev=None):
        self.w = None
        self.rd = []
        self.prev = prev or []


class V:
    __slots__ = ("ap", "trks")

    def __init__(self, ap, trks):
        self.ap = ap
        self.trks = trks

    def __getitem__(self, idx):
        return V(self.ap[idx], self.trks)


class Tile:
    def __init__(self, sch, name, shape, dtype, psum=False):
        nc = sch.nc
        self.h = nc.alloc_psum_tensor(name, list(shape), dtype) if psum else nc.alloc_sbuf_tensor(name, list(shape), dtype)
        self.base = Trk()
        self.parts = {}
        self.shape = shape

    def __getitem__(self, idx):
        return V(self.h[idx], [self.base])

    def p(self, key):
        if key not in self.parts:
            self.parts[key] = Trk()
        return _Part(self, self.parts[key])

    def all(self):
        return _All(self)


class Sub:
    def __init__(self, ap, prev=None):
        self.h = ap
        self.base = Trk(prev)

    def __getitem__(self, idx):
        return V(self.h[idx], [self.base])

    def ops(self):
        r = list(self.base.rd) + list(self.base.prev)
        if self.base.w is not None:
            r.append(self.base.w)
        return r


class AliasT:
    def __init__(self, tile, ap):
        self.h = ap
        self.base = tile.base

    def __getitem__(self, idx):
        return V(self.h[idx], [self.base])


class Arena:
    def __init__(self, sch, nbytes, name):
        self.t = sch.tile([128, nbytes // 4], F32, name)
        self.nbytes = nbytes
        self.subs = []
        self.off = 0
        self.prev = []

    def reset(self):
        prev = []
        seen = set()
        for s_ in self.subs:
            for o in s_.ops():
                if id(o) not in seen:
                    seen.add(id(o))
                    prev.append(o)
        for o in self.prev:
            if id(o) not in seen:
                seen.add(id(o))
                prev.append(o)
        self.prev = prev
        self.subs = []
        self.off = 0

    def get(self, shape, dtype):
        esz = 4 if dtype in (F32, I32) else 2
        n = 1
        for s_ in shape[1:]:
            n *= s_
        nb = (n * esz + 31) // 32 * 32
        assert self.off + nb <= self.nbytes, ("arena overflow", self.off, nb, self.nbytes)
        ap = self.t.h[:, self.off // 4:(self.off + nb) // 4]
        if esz == 2:
            ap = ap.bitcast(dtype)
        elif dtype != F32:
            ap = ap.bitcast(dtype)
        ap = ap[0:shape[0], 0:n]
        if len(shape) > 2:
            names = " ".join("a%d" % i for i in range(len(shape) - 1))
            kw = {"a%d" % i: shape[i + 1] for i in range(len(shape) - 1)}
            ap = ap.rearrange("p (%s) -> p %s" % (names, names), **kw)
        self.off += nb
        sub = Sub(ap, list(self.prev))
        self.subs.append(sub)
        return sub


class _Part:
    def __init__(self, t, trk):
        self.t, self.trk = t, trk

    def __getitem__(self, idx):
        return V(self.t.h[idx], [self.trk])


class _All:
    def __init__(self, t):
        self.t = t

    def __getitem__(self, idx):
        return V(self.t.h[idx], [self.t.base] + list(self.t.parts.values()))


ENGS = ("pe", "dve", "act", "pool", "sp")
NDS = 12


class Sched:
    def __init__(self, nc):
        self.nc = nc
        self.ops = []
        self.e = {"pe": nc.tensor, "dve": nc.vector, "act": nc.scalar, "pool": nc.gpsimd, "sp": nc.sync}
        self.sem = {k: nc.alloc_semaphore("s_" + k) for k in ENGS}
        self.dsems = {q: [nc.alloc_semaphore("d_%s%d" % (q, i)) for i in range(NDS)] for q in ("sp", "pool", "act")}
        self.dcount = {"sp": 0, "pool": 0, "act": 0}
        self.dlast = {"sp": {}, "pool": {}, "act": {}}
        self.nt = 0

    def tile(self, shape, dtype, name=None, psum=False):
        self.nt += 1
        return Tile(self, (name or "t") + "_%d" % self.nt, shape, dtype, psum)

    def add(self, eng, fn, reads=(), writes=(), dma=False, cc=False):
        op = Op()
        op.eng, op.fn, op.dma = eng, fn, dma
        op.cc = cc
        op.dprev = None
        op.inc = False
        op.idx = 0
        deps = {}
        for v in reads:
            for t in v.trks:
                if t.w is not None:
                    deps[id(t.w)] = (t.w, "raw")
        for v in writes:
            for t in v.trks:
                if t.w is not None and id(t.w) not in deps:
                    deps[id(t.w)] = (t.w, "waw")
                for r in t.rd:
                    if id(r) not in deps:
                        deps[id(r)] = (r, "war")
                for r in t.prev:
                    if id(r) not in deps:
                        deps[id(r)] = (r, "waw")
                t.prev = []
        fl = []
        for d, kind in deps.values():
            if d is op:
                continue
            if (not d.dma) and (not dma) and d.eng == eng:
                if eng == "pe":
                    continue
                if kind == "war":
                    continue
            fl.append(d)
            if not d.dma:
                d.inc = True
        op.deps = fl
        if cc:
            op.dsem = self.nc.alloc_semaphore("cc%d" % len(self.ops))
            op.dval = 1
        elif dma:
            i = self.dcount[eng]
            self.dcount[eng] += 1
            op.dsem = self.dsems[eng][i % NDS]
            op.dval = 16 * (i // NDS + 1)
            op.dprev = self.dlast[eng].get(i % NDS)
            self.dlast[eng][i % NDS] = op
        for v in writes:
            for t in v.trks:
                t.w = op
                t.rd = []
        for v in reads:
            for t in v.trks:
                if t.w is op:
                    continue
                if not dma:
                    t.rd = [r for r in t.rd if r.dma or r.eng != eng]
                t.rd.append(op)
        self.ops.append(op)
        return op

    def emit(self):
        cnt = {k: 0 for k in ENGS}
        for op in self.ops:
            if op.inc:
                cnt[op.eng] += 1
                op.idx = cnt[op.eng]
        waited = {k: {} for k in ENGS}
        for op in self.ops:
            e = self.e[op.eng]
            w = waited[op.eng]
            need = {}
            for d in op.deps:
                if d.dma:
                    key, val = d.dsem, d.dval
                else:
                    key, val = self.sem[d.eng], d.idx
                if need.get(key, 0) < val:
                    need[key] = val
            if op.dma and op.dprev is not None:
                key, val = op.dprev.dsem, op.dprev.dval
                if need.get(key, 0) < val:
                    need[key] = val
            for key, val in need.items():
                if w.get(key, 0) < val:
                    e.wait_ge(key, val)
                    w[key] = val
            ins = op.fn(e)
            if op.cc:
                ins.then_inc(op.dsem)
            elif op.dma:
                ins.then_inc(op.dsem, 16)
            elif op.inc:
                ins.then_inc(self.sem[op.eng], 1)
        sp = self.e["sp"]
        for q in ("sp", "pool", "act"):
            for op in self.dlast[q].values():
                sp.wait_ge(op.dsem, op.dval)
        return len(self.ops)

    def mm(self, out, lhsT, rhs, start=True, stop=True):
        return self.add("pe", lambda e: e.matmul(out.ap, lhsT=lhsT.ap, rhs=rhs.ap, start=start, stop=stop), [lhsT, rhs], [out])

    def tr(self, out, in_, ident):
        return self.add("pe", lambda e: e.transpose(out.ap, in_.ap, ident.ap), [in_, ident], [out])

    def act(self, out, in_, func, bias=None, scale=None, accum=None, eng="act"):
        rd = [in_]
        kw = {}
        if bias is not None:
            if isinstance(bias, V):
                rd.append(bias)
                kw["bias"] = bias.ap
            else:
                kw["bias"] = bias
        if scale is not None:
            if isinstance(scale, V):
                rd.append(scale)
                kw["scale"] = scale.ap
            else:
                kw["scale"] = scale
        wr = [out]
        if accum is not None:
            wr.append(accum)
            kw["accum_out"] = accum.ap
        return self.add("act", lambda e: e.activation(out=out.ap, in_=in_.ap, func=func, **kw), rd, wr)

    def tt(self, eng, out, a, b, op):
        return self.add(eng, lambda e: e.tensor_tensor(out=out.ap, in0=a.ap, in1=b.ap, op=op), [a, b], [out])

    def ts(self, eng, out, a, s1, s2=None, op0=ALU.mult, op1=None, accum=None):
        rd = [a]
        s1a = s1.ap if isinstance(s1, V) else s1
        s2a = s2.ap if isinstance(s2, V) else s2
        if isinstance(s1, V):
            rd.append(s1)
        if isinstance(s2, V):
            rd.append(s2)
        wr = [out]
        kw = {}
        if op1 is not None:
            kw["op1"] = op1
        if accum is not None:
            wr.append(accum)
            kw["accum_out"] = accum.ap
        return self.add(eng, lambda e: e.tensor_scalar(out=out.ap, in0=a.ap, scalar1=s1a, scalar2=s2a, op0=op0, **kw), rd, wr)

    def stt(self, out, a, s, b, op0, op1):
        rd = [a, b]
        sa = s.ap if isinstance(s, V) else s
        if isinstance(s, V):
            rd.append(s)
        return self.add("dve", lambda e: e.scalar_tensor_tensor(out=out.ap, in0=a.ap, scalar=sa, in1=b.ap, op0=op0, op1=op1), rd, [out])

    def copy(self, eng, out, in_):
        if eng == "act":
            return self.add("act", lambda e: e.copy(out=out.ap, in_=in_.ap), [in_], [out])
        return self.add(eng, lambda e: e.tensor_copy(out=out.ap, in_=in_.ap), [in_], [out])

    def recip(self, out, in_):
        return self.add("dve", lambda e: e.reciprocal(out=out.ap, in_=in_.ap), [in_], [out])

    def memset(self, eng, out, val):
        return self.add(eng, lambda e: e.memset(out.ap, val), [], [out])

    def reduce(self, out, in_, op, axis=AX.X):
        return self.add("dve", lambda e: e.tensor_reduce(out=out.ap, in_=in_.ap, axis=axis, op=op), [in_], [out])

    def dma_in(self, q, out, src, **kw):
        if isinstance(src, V):
            return self.add(q, lambda e: e.dma_start(out=out.ap, in_=src.ap, **kw), [src], [out], dma=True)
        return self.add(q, lambda e: e.dma_start(out=out.ap, in_=src, **kw), [], [out], dma=True)

    def dma_out(self, q, dst, in_, **kw):
        if isinstance(dst, V):
            return self.add(q, lambda e: e.dma_start(out=dst.ap, in_=in_.ap, **kw), [in_], [dst], dma=True)
        return self.add(q, lambda e: e.dma_start(out=dst, in_=in_.ap, **kw), [in_], [], dma=True)

    def gather(self, out, src, idx, bound):
        regs = self.__dict__.setdefault("_bregs", {})

        def fn(e):
            if bound not in regs:
                regs[bound] = e.to_reg(bound)
            return e.indirect_dma_start(
                out=out.ap, out_offset=None, in_=src.ap,
                in_offset=bass.IndirectOffsetOnAxis(ap=idx.ap, axis=0), bounds_check=regs[bound], oob_is_err=False)
        return self.add("pool", fn, [idx, src], [out], dma=True)

    def allgather(self, out, in_, ncores):
        return self.add("pool", lambda e: e.collective_compute(
            "AllGather", ALU.bypass, replica_groups=[list(range(ncores))],
            ins=[in_.ap.opt()], outs=[out.ap.opt()]), [in_], [out], dma=True, cc=True)


class Cfg:
    def __init__(self, T=4096, NPG=64, NPOOL=2560, NS=4, TS=4, topk_p=256, topk_s=256, NT=256, NCORES=8):
        self.T, self.NPG, self.NPOOL, self.NS, self.TS = T, NPG, NPOOL, NS, TS
        self.topk_p, self.topk_s = topk_p, topk_s
        self.NT = min(NT, T)
        self.NCH = T // self.NT
        self.PAST = NPG * 128
        self.NCORES = NCORES


PCOLS = {}
_off = 0
for _n, _c in (("norm1", 8), ("norm2", 8), ("pool_scale", 8), ("sconv_w", 4 * 24), ("sconv_b", 24), ("ssd_norm", 16),
               ("fconv_w", 3 * 44), ("fconv_b", 44), ("d_col", 16)):
    PCOLS[_n] = (_off, _c)
    _off += _c
NPCOL = _off


class Seg:
    pass


WSH = {"w_in": [L, D, DIN], "pool_w": [L, 4, 256, 256], "wb_attn": [L, D, D], "wb_pool": [L, D, D],
       "wb_ssd": [L, DSSD, D], "w_out": [L, D, D], "ffn_up": [L, D, 2 * DFF], "ffn_down": [L, DFF, D]}


def build(cfg):
    nc = bass.Bass("TRN2", target_bir_lowering=False)
    S = Sched(nc)
    T, NT, NCH, NS, TS, NPG = cfg.T, cfg.NT, cfg.NCH, cfg.NS, cfg.TS, cfg.NPG
    NSQ = 1 + NS
    NST = NS * TS
    PAST = cfg.PAST
    NKS = PAST + 128
    NKMAX = max(T, NKS)

    def din(name, shape, dt=F32):
        return nc.dram_tensor(name, list(shape), dt, kind="ExternalInput").ap()

    def dout(name, shape, dt=F32):
        return nc.dram_tensor(name, list(shape), dt, kind="ExternalOutput").ap()

    def dscr(name, shape, dt):
        return nc.dram_tensor(name, list(shape), dt, kind="Internal").ap()

    xp = din("xp", [T, D])
    xs = din("xs", [NST, D])
    call = din("call", [NSQ, D])
    NCO = cfg.NCORES

    GATH = []

    def gathered(name, rows, cols, nparts=1, dt=BF16):
        rs = rows // NCO
        sh = din(name + "_sh", [nparts * rs, cols])
        bnc = V(dscr(name + "_bin", [nparts * rs, cols], dt), [Trk()])
        fulls = [V(dscr(name + "_full%d" % q, [rows, cols], dt), [Trk()]) for q in range(nparts)]
        GATH.append((sh, bnc, fulls, rs, cols, nparts, dt))
        return fulls

    wbf = {}
    for nm_, shp_ in WSH.items():
        rows_ = 1
        for s_ in shp_[:-1]:
            rows_ *= s_
        f_ = gathered(nm_, rows_, shp_[-1])[0]
        if len(shp_) == 3:
            wbf[nm_] = V(f_.ap.rearrange("(l r) c -> l r c", l=L), f_.trks)
        else:
            wbf[nm_] = V(f_.ap.rearrange("(l g r) c -> l g r c", l=L, g=4), f_.trks)
    w_ada_full = gathered("w_ada", L * D, 6 * D, 1, F32)[0]
    w_ada = V(w_ada_full.ap.rearrange("(l k) c -> l k c", l=L), w_ada_full.trks)
    PCS = 2
    PR = cfg.NPOOL * 128 // PCS
    ck = gathered("ck", PR, 256, L * PCS)
    cv = gathered("cv", PR, 256, L * PCS)
    cki = gathered("cki", cfg.NPOOL * 128, 64, L)
    pt = din("pt", [1, NS * NPG], I32)
    st_pool = din("st_pool", [L, NS, 15, D])
    st_sconv = din("st_sconv", [L, NS, 3, DCONV])
    st_ssd = din("st_ssd", [L, NS, DSSD, SN])
    st_fconv = din("st_fconv", [L, NS, 2, 2 * DFF])
    b_ada = din("b_ada", [L, 1, 6 * D])
    ptab = din("ptab", [L, 128, NPCOL])
    frow = din("frow", [L, 1, 192])
    consts = din("consts", [128, 4, 128])
    rope_p = din("rope_p", [T, 64])
    rope_s = din("rope_s", [TS, 64])
    cntfix = din("cntfix", [128, 4, 16])

    y_p = dout("y_p", [T, D])
    y_s = dout("y_s", [NST, D])
    nk_p = dout("nk_p", [L, T, 256])
    nv_p = dout("nv_p", [L, T, 256])
    nki_p = dout("nki_p", [L, T, 64])
    npool_p = dout("npool_p", [L, 15, D])
    nsconv_p = dout("nsconv_p", [L, 3, DCONV])
    nssd_p = dout("nssd_p", [L, DSSD, SN])
    nfconv_p = dout("nfconv_p", [L, 2, 2 * DFF])
    nk_s = dout("nk_s", [L, NST, 256])
    nv_s = dout("nv_s", [L, NST, 256])
    nki_s = dout("nki_s", [L, NST, 64])
    npool_s = dout("npool_s", [L, NS, 15, D])
    nsconv_s = dout("nsconv_s", [L, NS, 3, DCONV])
    nssd_s = dout("nssd_s", [L, NS, DSSD, SN])
    nfconv_s = dout("nfconv_s", [L, NS, 2, 2 * DFF])

    pkt = [V(dscr("pkt%d" % l, [64, 4, T], BF16), [Trk()]) for l in range(L)]
    pkit = [V(dscr("pkit%d" % l, [64, T], BF16), [Trk()]) for l in range(L)]
    pva = [V(dscr("pva%d" % l, [T, 4, 65], BF16), [Trk()]) for l in range(L)]
    skt = [[V(dscr("skt%d_%d" % (l, s), [64, 4, NKS], BF16), [Trk()]) for s in range(NS)] for l in range(L)]
    skit = [[V(dscr("skit%d_%d" % (l, s), [64, NKS], BF16), [Trk()]) for s in range(NS)] for l in range(L)]
    sva = [[V(dscr("sva%d_%d" % (l, s), [NKS, 4, 65], BF16), [Trk()]) for s in range(NS)] for l in range(L)]

    AR = Arena(S, 57344, "arena")

    cst = S.tile([128, 4, 128], F32, "cst")
    S.dma_in("sp", cst[:], consts)
    ident = cst[:, 0, :]
    tri = cst[:, 1, :]
    negsl = cst[:, 2, :]
    negqk = cst[:, 3, :]
    identb_t = S.tile([128, 128], BF16, "identb")
    S.copy("dve", identb_t[:], ident)
    identb = identb_t[:]
    ones_t = S.tile([128, 128], F32, "ones")
    S.memset("dve", ones_t[:], 1.0)
    ones = ones_t[:]
    cst_eps = S.tile([128, 1], F32, "eps")
    S.memset("dve", cst_eps[:], EPS)
    cfix = S.tile([128, 4, 16], F32, "cfix")
    S.dma_in("sp", cfix[:], cntfix)
    pw2 = S.tile([128, NBIS + 2], F32, "pw2")
    for k in range(NBIS + 2):
        S.memset("pool", pw2[:, k:k + 1], 2.0 ** (-(k + 1)))
    ptb = S.tile([128, L, NPCOL], F32, "ptb")
    frw = S.tile([128, L, 192], F32, "frw")
    Abc = S.tile([128, L, 32], F32, "Abc")
    for l in range(L):
        S.dma_in("sp", ptb[:, l, :], ptab[l])
        S.dma_in("sp", frw[:, l, :], frow[l].to_broadcast([128, 192]))
        S.act(Abc[:, l, :], frw[:, l, 160:192], AF.Exp)
        S.ts("dve", Abc[:, l, :], Abc[:, l, :], -1.0)

    def pcol(l, name, j=0, n=1):
        o, c = PCOLS[name]
        return ptb[:, l, o + j:o + j + n]

    ptbc = S.tile([128, NS * NPG], I32, "ptbc")
    S.dma_in("sp", ptbc[:], pt.to_broadcast([128, NS * NPG]))
    iot = S.tile([128, 1], I32, "iot")
    S.add("pool", lambda e: e.iota(iot.h[:], pattern=[[0, 1]], base=0, channel_multiplier=1), [], [iot[:]])
    idxall = S.tile([128, PCS, NS * NPG], I32, "idxall")
    S.ts("dve", idxall[:, 0, :], ptbc[:], 128, iot[:, 0:1], op0=ALU.mult, op1=ALU.add)
    for h_ in range(1, PCS):
        S.ts("dve", idxall[:, h_, :], idxall[:, 0, :], -float(h_ * PR), None, op0=ALU.add)

    CV = 8192
    AR.reset()
    cvb = [AR.get([128, CV], BF16) for _ in range(2)]
    ci = 0
    for (sh, bnc, fulls, rs, cols, nparts, dt) in GATH:
        tot = nparts * rs * cols
        f1 = sh.rearrange("r c -> (r c)")
        f2 = bnc.ap.rearrange("r c -> (r c)")
        if dt == BF16:
            per = tot // 128
            src = f1.rearrange("(p f) -> p f", p=128)
            dst = f2.rearrange("(p f) -> p f", p=128)
            for f0 in range(0, per, CV):
                fn_ = min(CV, per - f0)
                bb = cvb[ci % 2]
                ci += 1
                S.dma_in("pool", bb[:, 0:fn_], src[:, f0:f0 + fn_])
                S.dma_out("sp", V(dst[:, f0:f0 + fn_], bnc.trks), bb[:, 0:fn_])
        else:
            ncp = 8
            per = tot // ncp
            for i in range(ncp):
                src_ = f1[i * per:(i + 1) * per].rearrange("(p f) -> p f", p=128)
                dst_ = f2[i * per:(i + 1) * per].rearrange("(p f) -> p f", p=128)
                S.add("pool", (lambda e, s_=src_, d_=dst_: e.dma_start(out=d_, in_=s_)), [], [bnc], dma=True)
        for q in range(nparts):
            S.allgather(fulls[q], V(bnc.ap[q * rs:(q + 1) * rs, :], bnc.trks), NCO)

    PS = [S.tile([128, 512], F32, "ps%d" % i, psum=True) for i in range(8)]
    psb = [0]

    def bank():
        b = PS[psb[0] % 4]
        psb[0] += 1
        return b

    def bfv(b):
        return Sub_shared(b)

    class Sub_shared:
        def __init__(self, b):
            self.b = b
            self.ap0 = b.h[:, :].bitcast(BF16)

        def __getitem__(self, idx):
            return V(self.ap0[idx], [self.b.base])

    WB = [S.tile([128, 4096], BF16, "wb%d" % i) for i in range(3)]
    wbi = [0]
    first_w = [True]

    def wload(src3, kc, ncol, rows=128):
        b = WB[wbi[0] % 3]
        wbi[0] += 1
        v = V(b.h[0:rows, 0:kc * ncol].rearrange("p (k c) -> p k c", k=kc), [b.base])
        S.add("sp", lambda e: e.dma_start(out=v.ap, in_=src3.ap), [src3], [v], dma=True)
        return v

    def wsrc(name, l, r0, kc, c0, ncol, rows=128):
        a = wbf[name].ap[l]
        return V(a[r0:r0 + kc * rows, c0:c0 + ncol].rearrange("(k p) c -> p k c", p=rows), wbf[name].trks)

    modT = S.tile([128, L, 48, NSQ], F32, "modT")
    amod = S.tile([128, L, 2, 8, NSQ], F32, "amod")
    AR.reset()
    adaw = [AR.get([128, 8, 512], F32) for _ in range(2)]
    ct = AR.get([NSQ, D], F32)
    cs_ = AR.get([NSQ, D], F32)
    cT = AR.get([128, 8, NSQ], F32)
    modtok = AR.get([NSQ, 512], F32)
    bada = AR.get([NSQ, 512], F32)
    S.dma_in("sp", ct[:], call)
    S.act(cs_[:], ct[:], AF.Silu)
    for kc in range(8):
        pb = bank()
        S.tr(pb[:, 0:NSQ], cs_[:, kc * 128:(kc + 1) * 128], ident[0:NSQ, 0:NSQ])
        S.copy("dve", cT[:, kc, :], pb[:, 0:NSQ])
    ai = 0
    for l in range(L):
        for cb in range(12):
            wt = adaw[ai % 2]
            ai += 1
            S.dma_in("sp", wt[:], V(w_ada.ap[l][:, cb * 512:(cb + 1) * 512].rearrange("(k p) c -> p k c", p=128), w_ada.trks))
            S.dma_in("sp", bada[:], b_ada[l][:, cb * 512:(cb + 1) * 512].to_broadcast([NSQ, 512]))
            pb = bank()
            for kc in range(8):
                S.mm(pb[0:NSQ, :], cT[:, kc, :], wt[:, kc, :], start=(kc == 0), stop=(kc == 7))
            S.tt("dve", modtok[:], pb[0:NSQ, :], bada[:], ALU.add)
            pb2 = bank()
            for j in range(4):
                S.tr(pb2[:, j * NSQ:(j + 1) * NSQ], modtok[:, j * 128:(j + 1) * 128], ident[0:NSQ, 0:NSQ])
            S.copy("act", modT[:, l, cb * 4:cb * 4 + 4, :],
                   V(pb2.h[:, 0:4 * NSQ].rearrange("p (j s) -> p j s", j=4), [pb2.base]))
        for wn, (sc0, nm) in enumerate(((8, "norm1"), (32, "norm2"))):
            for j in range(8):
                S.ts("dve", amod[:, l, wn, j, :], modT[:, l, sc0 + j, :], 1.0, pcol(l, nm, j), op0=ALU.add, op1=ALU.mult)

    def mod_shift(l, wn, j, s):
        return modT[:, l, (0 if wn == 0 else 24) + j, s:s + 1]

    def mod_gate(l, wn, j, s):
        return modT[:, l, (16 if wn == 0 else 40) + j, s:s + 1]

    def halo_load(dst_tile, src2d, C, h):
        AR.reset()
        stg_ = AR.get([16, 5632], F32)
        S.dma_in("sp", stg_[0:h, 0:C * 128], src2d)
        pb = bank()
        for c in range(C):
            S.tr(pb[:, c * h:(c + 1) * h], stg_[0:h, c * 128:(c + 1) * 128], ident[0:h, 0:h])
        S.copy("act", dst_tile[:, :, :], V(pb.h[:, 0:C * h].rearrange("p (c t) -> p c t", c=C), [pb.base]))

    def halo_store(dst2d, src_tile, C, h):
        AR.reset()
        stg_ = AR.get([16, 5632], F32)
        for c0 in range(0, C, 4):
            cn = min(4, C - c0)
            pb = bank()
            for cc in range(cn):
                S.tr(pb[0:h, cc * 128:(cc + 1) * 128], src_tile[:, c0 + cc, :], ident)
            S.copy("act", stg_[0:h, c0 * 128:(c0 + cn) * 128], pb[0:h, 0:cn * 128])
        S.dma_out("pool", dst2d, stg_[0:h, 0:C * 128])

    class SeqState:
        pass

    def new_state(name):
        st = SeqState()
        st.hs = [S.tile([128, SH * SP], F32, "%s_hs%d" % (name, l)) for l in range(L)]
        st.pool_h = [S.tile([128, 8, 15], F32, "%s_ph%d" % (name, l)) for l in range(L)]
        st.sconv_h = [S.tile([128, 24, 3], F32, "%s_sh%d" % (name, l)) for l in range(L)]
        st.fconv_h = [S.tile([128, 44, 2], F32, "%s_fh%d" % (name, l)) for l in range(L)]
        return st

    pstate = new_state("p")
    for l in range(L):
        S.memset("pool", pstate.hs[l][:], 0.0)
        S.memset("pool", pstate.pool_h[l][:], 0.0)
        S.memset("pool", pstate.sconv_h[l][:], 0.0)
        S.memset("pool", pstate.fconv_h[l][:], 0.0)
    sstates = []
    hs_shared = S.tile([128, SH * SP], F32, "s_hs")
    for s in range(NS):
        st = SeqState()
        st.hs = [hs_shared for l in range(L)]
        st.pool_h = [S.tile([128, 8, 15], F32, "s%d_ph%d" % (s, l)) for l in range(L)]
        st.sconv_h = [S.tile([128, 24, 3], F32, "s%d_sh%d" % (s, l)) for l in range(L)]
        st.fconv_h = [S.tile([128, 44, 2], F32, "s%d_fh%d" % (s, l)) for l in range(L)]
        sstates.append(st)
        for l in range(L):
            halo_load(st.pool_h[l], st_pool[l, s], 8, 15)
            halo_load(st.sconv_h[l], st_sconv[l, s], 24, 3)
            halo_load(st.fconv_h[l], st_fconv[l, s], 44, 2)

    xT = S.tile([128, 8, NT], F32, "xT")
    hT = S.tile([128, 8, NT], BF16, "hT")
    rstd = S.tile([128, NT], F32, "rstd")
    tmpn = S.tile([128, NT], F32, "tmpn")
    xtok = S.tile([128, D], F32, "xtok")
    qT = S.tile([128, NH, NT], BF16, "qT")
    qiT = S.tile([128, IH, NT], BF16, "qiT")
    oattnT = S.tile([64, NH, NT], BF16, "oattnT")
    opoolT = S.tile([128, 8, NT], BF16, "opoolT")
    ossdT = S.tile([128, 16, NT], BF16, "ossdT")
    NSEGMAX = max(NT // 128, NS)
    wsc = S.tile([128, NSEGMAX, 8], F32, "wsc")
    dtall = S.tile([128, NSEGMAX, 32], F32, "dtall")
    ropet = S.tile([128, 64], F32, "ropet")
    ssq = S.tile([128, 16], F32, "ssq")
    rq = S.tile([128, 16], F32, "rq")
    bis = S.tile([128, 8], F32, "bis")
    steps = S.tile([128, NBIS + 2], F32, "steps")
    maskT = S.tile([128, NKMAX // 128 + 1, 128 if T >= 128 else TS], BF16, "maskT") if False else \
        S.tile([128, max((T // 128) * 128, (NKS // 128) * TS)], BF16, "maskT")
    ktile = [S.tile([128, 4, 128], BF16, "ktile%d" % i) for i in range(3)]
    vtile = [S.tile([128, 4, 65], BF16, "vtile%d" % i) for i in range(3)]
    pexp = [S.tile([128, 512], BF16, "pexp%d" % i) for i in range(2)]
    pmsk = [S.tile([128, 512], BF16, "pmsk%d" % i) for i in range(2)]
    rden = S.tile([128, 512], F32, "rden")
    osb = S.tile([64, 512], F32, "osb")
    relu_b = [S.tile([128, 512], F32, "relu%d" % i) for i in range(2)]
    ones64 = S.tile([128, 64], F32, "ones64")
    S.memset("dve", ones64[:], 1.0)

    def rmsnorm_mod(l, wn, n, msegs):
        AR.reset()
        sq = AR.get([128, 8, NT], F32)
        for j in range(8):
            S.act(sq[:, j, 0:n], xT[:, j, 0:n], AF.Square)
        pb = bank()
        for j in range(8):
            S.mm(pb[:, 0:n], ones, sq[:, j, 0:n], start=(j == 0), stop=(j == 7))
        S.act(rstd[:, 0:n], pb[:, 0:n], AF.Sqrt, bias=cst_eps[:, 0:1], scale=1.0 / D)
        S.recip(rstd[:, 0:n], rstd[:, 0:n])
        for j in range(8):
            S.tt("dve", tmpn[:, 0:n], xT[:, j, 0:n], rstd[:, 0:n], ALU.mult)
            for (c0, sn, si) in msegs:
                S.act(hT[:, j, c0:c0 + sn], tmpn[:, c0:c0 + sn], AF.Identity,
                      bias=mod_shift(l, wn, j, si), scale=amod[:, l, wn, j, si:si + 1])

    def rope_norm(l, src, c0, nh, n, gain_off, bufs):
        w1, w2, w3 = bufs
        x3 = V(src.h[0:n, c0:c0 + nh * 64].rearrange("p (h d) -> p h d", d=64), [src.base])
        work = V(w1.h[0:n, 0:nh * 64].rearrange("p (h d) -> p h d", d=64), [w1.base])
        work2 = V(w2.h[0:n, 0:nh * 64].rearrange("p (h d) -> p h d", d=64), [w2.base])
        tA = V(w3.h[0:n, 0:nh * 32].rearrange("p (h d) -> p h d", d=32), [w3.base])
        if gain_off is not None:
            S.tt("dve", work, x3, x3, ALU.mult)
            S.reduce(ssq[0:n, 0:nh], work, ALU.add)
            S.act(rq[0:n, 0:nh], ssq[0:n, 0:nh], AF.Sqrt, bias=cst_eps[0:n, 0:1], scale=1.0 / 64)
            S.recip(rq[0:n, 0:nh], rq[0:n, 0:nh])
            rqb = V(rq.h[0:n, 0:nh].rearrange("p (h o) -> p h o", o=1).to_broadcast([n, nh, 64]), [rq.base])
            S.tt("dve", work, x3, rqb, ALU.mult)
            gb = V(frw.h[0:n, l, gain_off:gain_off + 64].rearrange("p (o d) -> p o d", o=1).to_broadcast([n, nh, 64]), [frw.base])
            S.tt("dve", work, work, gb, ALU.mult)
            xin = work
        else:
            xin = x3
        cosb = V(ropet.h[0:n, 0:32].rearrange("p (o d) -> p o d", o=1).to_broadcast([n, nh, 32]), [ropet.base])
        sinb = V(ropet.h[0:n, 32:64].rearrange("p (o d) -> p o d", o=1).to_broadcast([n, nh, 32]), [ropet.base])
        x1 = V(xin.ap[:, :, 0:32], xin.trks)
        x2 = V(xin.ap[:, :, 32:64], xin.trks)
        o1 = V(work2.ap[:, :, 0:32], work2.trks)
        o2 = V(work2.ap[:, :, 32:64], work2.trks)
        S.tt("dve", o1, x1, cosb, ALU.mult)
        S.tt("pool", tA, x2, sinb, ALU.mult)
        S.tt("dve", o1, o1, tA, ALU.subtract)
        S.tt("dve", o2, x1, sinb, ALU.mult)
        S.tt("pool", tA, x2, cosb, ALU.mult)
        S.tt("dve", o2, o2, tA, ALU.add)
        return V(w2.h[0:n, 0:nh * 64], [w2.base])

    def stage_B(l, segs):
        hl = 64 * l
        AR.reset()
        stg = [AR.get([128, 640], F32) for _ in range(2)]
        stb = [AR.get([128, 640], BF16) for _ in range(2)]
        w1 = AR.get([128, 512], F32)
        w2 = AR.get([128, 512], F32)
        w3 = AR.get([128, 256], F32)
        bufs = (w1, w2, w3)
        vb = AR.get([128, 4, 65], BF16)
        blocks = (("q0", O_Q, 512), ("q1", O_Q + 512, 512), ("kv", O_K, 512), ("qi", O_QI, 512), ("kiw", O_KI, 72), ("dt", O_DT, 32))
        si_ = 0
        for (bn, c0w, ncol) in blocks:
            wv = wload(wsrc("w_in", l, 0, 8, c0w, ncol), 8, ncol)
            for sg in segs:
                n = sg.n
                cs0 = sg.c0
                sgt = stg[si_ % 2]
                sbt = stb[si_ % 2]
                si_ += 1
                pb = bank()
                for kc in range(8):
                    S.mm(pb[0:n, 0:ncol], hT[:, kc, cs0:cs0 + n], wv[:, kc, :], start=(kc == 0), stop=(kc == 7))
                S.copy("act", sgt[0:n, 64:64 + ncol], pb[0:n, 0:ncol])
                if bn in ("q0", "q1", "qi", "kv", "kiw") and True:
                    if bn != "dt":
                        pass
                if bn == "q0" or bn == "q1" or bn == "qi" or bn == "kv" or bn == "kiw":
                    S.dma_in("sp", ropet[0:n, :], sg.rope)
                if bn in ("q0", "q1"):
                    res = rope_norm(l, sgt, 64, 8, n, 0, bufs)
                    S.copy("act", sbt[0:n, 64:64 + 512], res)
                    h0 = 0 if bn == "q0" else 8
                    for hb in range(2):
                        pbt = bank()
                        pbv = bfv(pbt)
                        for hh in range(4):
                            h = hb * 4 + hh
                            S.tr(pbv[:, hh * n:(hh + 1) * n], sbt[0:n, 64 + 64 * h - hl:64 + 64 * h - hl + 128], identb[0:n, 0:n])
                        S.copy("dve", V(qT.h[hl:hl + 64, h0 + hb * 4:h0 + hb * 4 + 4, cs0:cs0 + n], [qT.base]),
                               V(pbv.ap0[hl:hl + 64, 0:4 * n].rearrange("p (h t) -> p h t", h=4), [pbt.base]))
                elif bn == "qi":
                    res = rope_norm(l, sgt, 64, 8, n, None, bufs)
                    S.copy("act", sbt[0:n, 64:64 + 512], res)
                    for hb in range(2):
                        pbt = bank()
                        pbv = bfv(pbt)
                        for hh in range(4):
                            h = hb * 4 + hh
                            S.tr(pbv[:, hh * n:(hh + 1) * n], sbt[0:n, 64 + 64 * h - hl:64 + 64 * h - hl + 128], identb[0:n, 0:n])
                        S.copy("dve", V(qiT.h[hl:hl + 64, hb * 4:hb * 4 + 4, cs0:cs0 + n], [qiT.base]),
                               V(pbv.ap0[hl:hl + 64, 0:4 * n].rearrange("p (h t) -> p h t", h=4), [pbt.base]))
                elif bn == "kv":
                    res = rope_norm(l, sgt, 64, 4, n, 64, bufs)
                    S.dma_out("pool", sg.nk_out(l), res)
                    S.dma_out("pool", sg.nv_out(l), sgt[0:n, 64 + 256:64 + 512])
                    S.copy("act", sbt[0:n, 64:64 + 256], res)
                    pbt = bank()
                    pbv = bfv(pbt)
                    for g in range(4):
                        S.tr(pbv[:, g * n:(g + 1) * n], sbt[0:n, 64 + 64 * g:64 + 64 * g + 128], identb[0:n, 0:n])
                    ktn = ktile[0]
                    S.copy("dve", V(ktn.h[0:64, :, 0:n], [ktn.base]),
                           V(pbv.ap0[0:64, 0:4 * n].rearrange("p (h t) -> p h t", h=4), [pbt.base]))
                    S.dma_out("pool", V(sg.kt.ap[:, :, sg.key0:sg.key0 + n], sg.kt.trks), V(ktn.h[0:64, :, 0:n], [ktn.base]))
                    S.memset("pool", vb[0:n, :, 64:65], 1.0)
                    S.copy("dve", vb[0:n, :, 0:64], V(sgt.h[0:n, 64 + 256:64 + 512].rearrange("p (g d) -> p g d", g=4), [sgt.base]))
                    S.dma_out("pool", V(sg.va.ap[sg.key0:sg.key0 + n, :, :], sg.va.trks), vb[0:n, :, :])
                elif bn == "kiw":
                    res = rope_norm(l, sgt, 64, 1, n, None, bufs)
                    S.dma_out("pool", sg.nki_out(l), res)
                    S.copy("act", sbt[0:n, 64:64 + 64], res)
                    pbt = bank()
                    pbv = bfv(pbt)
                    S.tr(pbv[:, 0:n], sbt[0:n, 64:64 + 128], identb[0:n, 0:n])
                    ktn = ktile[1]
                    S.copy("dve", V(ktn.h[0:64, 0, 0:n], [ktn.base]), pbv[0:64, 0:n])
                    S.dma_out("pool", V(sg.kit.ap[:, sg.key0:sg.key0 + n], sg.kit.trks), V(ktn.h[0:64, 0, 0:n], [ktn.base]))
                    S.ts("dve", wsc[0:n, sg.i, :], sgt[0:n, 64 + 64:64 + 72], float(IH) ** -0.5)
                elif bn == "dt":
                    S.tt("dve", sgt[0:n, 64:96], sgt[0:n, 64:96], frw[0:n, l, 128:160], ALU.add)
                    S.act(sgt[0:n, 64:96], sgt[0:n, 64:96], AF.Exp)
                    S.act(dtall[0:n, sg.i, :], sgt[0:n, 64:96], AF.Ln, bias=ones64[0:n, 0:1])

    def stage_C(l, sg):
        hl = 64 * l
        n = sg.n
        nkeys = sg.key0 + n
        topk = sg.topk
        nkt = (nkeys + 127) // 128
        AR.reset()
        isc = AR.get([128, NKMAX], F32)
        imk = AR.get([128, NKMAX], BF16)
        kitb = imk
        S.dma_in("sp", V(kitb.h[hl:hl + 64, 0:nkeys], [kitb.base]), V(sg.kit.ap[:, 0:nkeys], sg.kit.trks))
        ri = 0
        for k0 in range(0, nkeys, 512):
            kn = min(512, nkeys - k0)
            for h in range(IH):
                pb = bank()
                S.mm(pb[0:n, 0:kn], V(qiT.h[hl:hl + 64, h, sg.c0:sg.c0 + n], [qiT.base]),
                     V(kitb.h[hl:hl + 64, k0:k0 + kn], [kitb.base]))
                rb = relu_b[ri % 2]
                ri += 1
                S.act(rb[0:n, 0:kn], pb[0:n, 0:kn], AF.Relu, scale=0.125)
                if h == 0:
                    S.ts("dve", isc[0:n, k0:k0 + kn], rb[0:n, 0:kn], wsc[0:n, sg.i, 0:1])
                else:
                    S.stt(isc[0:n, k0:k0 + kn], rb[0:n, 0:kn], wsc[0:n, sg.i, h:h + 1], isc[0:n, k0:k0 + kn], ALU.mult, ALU.add)
        if nkeys > topk:
            S.reduce(bis[0:n, 0:1], isc[0:n, 0:nkeys], ALU.max)
            S.reduce(bis[0:n, 1:2], isc[0:n, 0:nkeys], ALU.min)
        S.tt("dve", isc[0:n, sg.key0:sg.key0 + n], isc[0:n, sg.key0:sg.key0 + n], negqk[0:n, 0:n], ALU.add)
        if nkeys <= topk:
            S.ts("dve", imk[0:n, 0:nkeys], isc[0:n, 0:nkeys], -1.0e29, None, op0=ALU.is_ge)
        else:
            S.ts("dve", bis[0:n, 0:1], bis[0:n, 0:1], 1.0e-3, None, op0=ALU.add)
            S.tt("dve", bis[0:n, 2:3], bis[0:n, 0:1], bis[0:n, 1:2], ALU.subtract)
            S.ts("dve", steps[0:n, :], pw2[0:n, :], bis[0:n, 2:3])
            S.tt("dve", bis[0:n, 3:4], bis[0:n, 1:2], steps[0:n, 0:1], ALU.add)
            for k in range(NBIS):
                S.ts("dve", imk[0:n, 0:nkeys], isc[0:n, 0:nkeys], bis[0:n, 3:4], 0.0, op0=ALU.is_ge, op1=ALU.add,
                     accum=bis[0:n, 4:5])
                S.ts("dve", bis[0:n, 5:6], bis[0:n, 4:5], topk - 0.5, 0.5, op0=ALU.is_ge, op1=ALU.subtract)
                S.stt(bis[0:n, 3:4], bis[0:n, 5:6], steps[0:n, k:k + 1], bis[0:n, 3:4], ALU.mult, ALU.add)
            S.tt("dve", bis[0:n, 6:7], bis[0:n, 3:4], steps[0:n, NBIS:NBIS + 1], ALU.subtract)
            S.ts("dve", bis[0:n, 6:7], bis[0:n, 6:7], -1.0e29, None, op0=ALU.max)
            S.ts("dve", imk[0:n, 0:nkeys], isc[0:n, 0:nkeys], bis[0:n, 6:7], None, op0=ALU.is_ge)
        for j0 in range(0, nkt, 4):
            pbt = bank()
            pbv = bfv(pbt)
            jn = min(4, nkt - j0)
            nks = []
            for jj in range(jn):
                j = j0 + jj
                nk = min(128, nkeys - j * 128)
                nks.append(nk)
                S.tr(pbv[0:nk, jj * n:(jj + 1) * n], imk[0:n, j * 128:j * 128 + nk], identb[0:n, 0:n])
            if all(k_ == 128 for k_ in nks):
                S.copy("act", maskT[:, j0 * n:(j0 + jn) * n], pbv[:, 0:jn * n])
            else:
                for jj in range(jn):
                    S.copy("act", maskT[0:nks[jj], (j0 + jj) * n:(j0 + jj + 1) * n], pbv[0:nks[jj], jj * n:(jj + 1) * n])
        for j in range(nkt):
            nk = min(128, nkeys - j * 128)
            kt_ = ktile[j % 3]
            vt_ = vtile[j % 3]
            S.dma_in("sp", V(kt_.h[hl:hl + 64, :, 0:nk], [kt_.base]), V(sg.kt.ap[:, :, j * 128:j * 128 + nk], sg.kt.trks))
            S.dma_in("sp", vt_[0:nk, :, :], V(sg.va.ap[j * 128:j * 128 + nk, :, :], sg.va.trks))
            for g in range(4):
                pb = bank()
                S.mm(pb[0:nk, 0:4 * n], V(kt_.h[hl:hl + 64, g, 0:nk], [kt_.base]),
                     V(qT.h[hl:hl + 64, 4 * g:4 * g + 4, sg.c0:sg.c0 + n], [qT.base]))
                pe_ = pexp[(j * 4 + g) % 2]
                pm_ = pmsk[(j * 4 + g) % 2]
                S.act(pe_[0:nk, 0:4 * n], pb[0:nk, 0:4 * n], AF.Exp, scale=float(HD) ** -0.5)
                mk = V(maskT.h[0:nk, j * n:(j + 1) * n].rearrange("p (o q) -> p o q", o=1).to_broadcast([nk, 4, n]), [maskT.base])
                S.tt("dve", V(pm_.h[0:nk, 0:4 * n].rearrange("p (h q) -> p h q", h=4), [pm_.base]),
                     V(pe_.h[0:nk, 0:4 * n].rearrange("p (h q) -> p h q", h=4), [pe_.base]), mk, ALU.mult)
                S.mm(PS[4 + g][0:65, 0:4 * n], vt_[0:nk, g, :], pm_[0:nk, 0:4 * n], start=(j == 0), stop=(j == nkt - 1))
        for g in range(4):
            acc = PS[4 + g]
            S.recip(rden[64:65, 0:4 * n], acc[64:65, 0:4 * n])
            pb = bank()
            S.mm(pb[0:64, 0:4 * n], ones64[64:65, 0:64], rden[64:65, 0:4 * n])
            S.copy("act", osb[0:64, 0:4 * n], acc[0:64, 0:4 * n])
            S.tt("dve", V(oattnT.h[0:64, 4 * g:4 * g + 4, sg.c0:sg.c0 + n], [oattnT.base]),
                 V(osb.h[0:64, 0:4 * n].rearrange("p (h q) -> p h q", h=4), [osb.base]),
                 V(pb.h[0:64, 0:4 * n].rearrange("p (h q) -> p h q", h=4), [pb.base]), ALU.mult)

    def fm_proj(l, c0w, noc, n, consume, name="w_in", kc=8, blk=512):
        per = blk // 128
        for b0 in range(0, noc, per):
            nb = min(per, noc - b0)
            wv = wload(wsrc(name, l, 0, kc, c0w + b0 * 128, nb * 128), kc, nb * 128)
            for o in range(nb):
                pb = bank()
                for k in range(kc):
                    S.mm(pb[:, 0:n], wv[:, k, o * 128:(o + 1) * 128], hT[:, k, 0:n], start=(k == 0), stop=(k == kc - 1))
                consume(b0 + o, pb)

    def stage_D(l, n, csegs, first_prompt):
        AR.reset()
        nsg = len(csegs)
        sn = csegs[0][1]
        W = 15 + sn
        U = AR.get([128, 8, nsg, W], F32)
        pA = AR.get([128, 2, nsg, W], F32)
        pB = AR.get([128, 2, nsg, W], F32)
        pooled = AR.get([128, 8, n], BF16)
        for si, (c0, sn_, st) in enumerate(csegs):
            S.copy("pool", U[:, :, si, 0:15], st.pool_h[l][:, :, :])

        def cons(oc, pb):
            S.copy("act", U[:, oc, :, 15:15 + sn], V(pb.h[:, 0:n].rearrange("p (s t) -> p s t", s=nsg), [pb.base]))
        fm_proj(l, O_PU, 8, n, cons)
        for si, (c0, sn_, st) in enumerate(csegs):
            S.copy("pool", st.pool_h[l][:, :, :], U[:, :, si, sn:sn + 15])
        pw = wload(V(wbf["pool_w"].ap[l].rearrange("g (c p) d -> p (g c) d", p=128), wbf["pool_w"].trks), 8, 256)
        for gi, wnd in enumerate((2, 4, 8, 16)):
            src = V(U.h[:, 2 * gi:2 * gi + 2, :, :], [U.base])
            lo = 0
            sh = 1
            k = 0
            while sh < wnd:
                dst = pA if k % 2 == 0 else pB
                S.tt("pool", V(dst.h[:, :, :, lo + sh:W], [dst.base]), V(src.ap[:, :, :, lo + sh:W], src.trks),
                     V(src.ap[:, :, :, lo:W - sh], src.trks), ALU.add)
                src = V(dst.h[:, :, :, :], [dst.base])
                lo += sh
                sh *= 2
                k += 1
            if first_prompt:
                fx = V(cfix.h[:, gi, 0:15].rearrange("p (a b t) -> p a b t", a=1, b=1).to_broadcast([128, 2, nsg, 15]), [cfix.base])
                S.tt("pool", V(src.ap[:, :, :, 15:30], src.trks), V(src.ap[:, :, :, 15:30], src.trks), fx, ALU.mult)
            S.stt(V(pooled.h[:, 2 * gi:2 * gi + 2, 0:n].rearrange("p c (s t) -> p c s t", s=nsg), [pooled.base]),
                  V(src.ap[:, :, :, 15:15 + sn], src.trks), 1.0 / wnd,
                  V(U.h[:, 2 * gi:2 * gi + 2, :, 15:15 + sn], [U.base]), ALU.mult, ALU.subtract)
            for dc in range(2):
                pb = bank()
                for cc in range(2):
                    S.mm(pb[:, 0:n], pw[:, gi * 2 + cc, dc * 128:(dc + 1) * 128], pooled[:, 2 * gi + cc, 0:n],
                         start=(cc == 0), stop=(cc == 1))
                S.act(opoolT[:, 2 * gi + dc, 0:n], pb[:, 0:n], AF.Copy, scale=pcol(l, "pool_scale", 2 * gi + dc))

    def conv_fm(dst_f32, pb, n, csegs, halo_tiles, oc, hw, wname, bname, l, stgc):
        nsg = len(csegs)
        sn = csegs[0][1]
        for si, (c0, sn_, st) in enumerate(csegs):
            S.copy("pool", stgc[:, si, 0:hw], halo_tiles(st)[:, oc, :])
        S.copy("act", stgc[:, :, hw:hw + sn], V(pb.h[:, 0:n].rearrange("p (s t) -> p s t", s=nsg), [pb.base]))
        for si, (c0, sn_, st) in enumerate(csegs):
            S.copy("pool", halo_tiles(st)[:, oc, :], stgc[:, si, sn:sn + hw])
        dv = V(dst_f32.ap.rearrange("p (s t) -> p s t", s=nsg), dst_f32.trks)
        ntap = hw + 1
        wo, _ = PCOLS[wname]
        for j in range(ntap):
            wj = ptb[:, l, wo + j * (24 if ntap == 4 else 44) + oc:wo + j * (24 if ntap == 4 else 44) + oc + 1]
            if j == 0:
                S.ts("dve", dv, stgc[:, :, 0:sn], wj, pcol(l, bname, oc), op0=ALU.mult, op1=ALU.add)
            else:
                S.stt(dv, stgc[:, :, j:j + sn], wj, dv, ALU.mult, ALU.add)

    def stage_E(l, n, csegs, segs):
        AR.reset()
        nsg = len(csegs)
        sn = csegs[0][1]
        xbcT = AR.get([128, 24, NT], BF16)
        szT = AR.get([128, 16, NT], BF16)
        stgc = AR.get([128, nsg, 3 + sn], F32)
        cres = AR.get([128, NT], F32)

        def cons_z(oc, pb):
            S.act(szT[:, oc, 0:n], pb[:, 0:n], AF.Silu)
        fm_proj(l, O_Z, 16, n, cons_z)

        def cons_x(oc, pb):
            conv_fm(cres[:, 0:n], pb, n, csegs, lambda st: st.sconv_h[l], oc, 3, "sconv_w", "sconv_b", l, stgc)
            S.act(xbcT[:, oc, 0:n], cres[:, 0:n], AF.Silu)
        fm_proj(l, O_XBC, 24, n, cons_x)

        xdt = AR.get([128, 32, 64], BF16)
        xdtd = AR.get([128, 32, 64], BF16)
        Btok = AR.get([128, 4, 128], BF16)
        atok = AR.get([128, 32], F32)
        cstok = AR.get([128, 32], F32)
        dec = AR.get([128, 32], F32)
        cdec = AR.get([128, 32], F32)
        abc = AR.get([128, 8, 128], F32)
        Eg = AR.get([128, 8, 128], F32)
        Egb = AR.get([128, 8, 128], BF16)
        Mg = AR.get([128, 8, 128], BF16)
        Csg = AR.get([128, 8, 128], BF16)
        CBm = AR.get([128, 128], BF16)
        hbf = AR.get([128, 512], BF16)
        htmp = AR.get([128, 512], F32)
        y3 = AR.get([128, 4, 128], F32)
        ysq = AR.get([128, 4, 128], F32)
        rs = AR.get([128, 128], F32)
        for sg in segs:
            m = sg.n
            c0 = sg.c0
            st = sg.state
            hs = st.hs[l]
            if sg.load_state is not None:
                sg.load_state(l)
            for half in range(2):
                pbt = bank()
                pbv = bfv(pbt)
                for o in range(8):
                    S.tr(pbv[0:m, o * 128:(o + 1) * 128], xbcT[:, half * 8 + o, c0:c0 + m], identb)
                dtb = V(dtall.h[0:m, sg.i, half * 16:half * 16 + 16].rearrange("p (h o) -> p h o", o=1).to_broadcast([m, 16, 64]), [dtall.base])
                S.tt("dve", xdt[0:m, half * 16:half * 16 + 16, :],
                     V(pbv.ap0[0:m, 0:1024].rearrange("p (h d) -> p h d", d=64), [pbt.base]), dtb, ALU.mult)
            pbt = bank()
            pbv = bfv(pbt)
            for g in range(4):
                S.tr(pbv[0:m, g * 128:(g + 1) * 128], xbcT[:, 16 + g, c0:c0 + m], identb)
            S.copy("act", V(Btok.h[0:m, :, :], [Btok.base]), V(pbv.ap0[0:m, 0:512].rearrange("p (g s) -> p g s", g=4), [pbt.base]))
            S.tt("dve", atok[0:m, :], dtall[0:m, sg.i, :], Abc[0:m, l, :], ALU.mult)
            pb = bank()
            S.mm(pb[0:m, 0:32], tri[0:m, 0:m], atok[0:m, :])
            S.mm(pb[:, 32:64], ones[0:m, :], atok[0:m, :])
            S.copy("act", cstok[0:m, :], pb[0:m, 0:32])
            S.tt("dve", dec[0:m, :], pb[0:m, 32:64], cstok[0:m, :], ALU.subtract)
            S.act(dec[0:m, :], dec[0:m, :], AF.Exp)
            S.act(cdec[:, :], pb[:, 32:64], AF.Exp)
            decb = V(dec.h[0:m, :].rearrange("p (h o) -> p h o", o=1).to_broadcast([m, 32, 64]), [dec.base])
            S.tt("dve", xdtd[0:m, :, :], xdt[0:m, :, :], decb, ALU.mult)
            for g in range(4):
                ab = V(atok.h[0:m, 8 * g:8 * g + 8].rearrange("p (h o) -> p h o", o=1).to_broadcast([m, 8, 128]), [atok.base])
                S.copy("pool", abc[0:m, :, :], ab)
                pcs = [bank(), bank()]
                for hh in range(8):
                    S.mm(pcs[hh // 4][:, (hh % 4) * m:(hh % 4 + 1) * m], abc[0:m, hh, :], tri[0:m, 0:m])
                pb = bank()
                S.mm(pb[0:m, 0:m], xbcT[:, 16 + g, c0:c0 + m], xbcT[:, 20 + g, c0:c0 + m])
                S.tt("dve", CBm[0:m, 0:m], pb[0:m, 0:m], tri[0:m, 0:m], ALU.mult)
                for hh in range(8):
                    S.stt(Eg[0:m, hh, 0:m], pcs[hh // 4][0:m, (hh % 4) * m:(hh % 4 + 1) * m], cstok[0:m, 8 * g + hh:8 * g + hh + 1],
                          negsl[0:m, 0:m], ALU.subtract, ALU.min)
                S.act(Egb[0:m, :, 0:m], Eg[0:m, :, 0:m], AF.Exp)
                cbb = V(CBm.h[0:m, 0:m].rearrange("p (o q) -> p o q", o=1).to_broadcast([m, 8, m]), [CBm.base])
                S.tt("dve", Mg[0:m, :, 0:m], Egb[0:m, :, 0:m], cbb, ALU.mult)
                for q in range(2):
                    S.act(Eg[:, q * 4:q * 4 + 4, 0:m], V(pcs[q].h[:, 0:4 * m].rearrange("p (h t) -> p h t", h=4), [pcs[q].base]), AF.Exp)
                ctb = V(xbcT.h[:, 20 + g, c0:c0 + m].rearrange("p (o t) -> p o t", o=1).to_broadcast([128, 8, m]), [xbcT.base])
                S.tt("dve", Csg[:, :, 0:m], Eg[:, :, 0:m], ctb, ALU.mult)
                S.copy("act", hbf[:, :], hs[:, g * 512:(g + 1) * 512])
                yb = PS[4 + (g % 2)]
                for hh in range(8):
                    h = 8 * g + hh
                    po = 64 * (h % 2)
                    jj = hh // 2
                    S.mm(yb[po:po + 64, jj * m:(jj + 1) * m], xdt[0:m, h, :], Mg[0:m, hh, 0:m], start=True, stop=False)
                    S.mm(yb[po:po + 64, jj * m:(jj + 1) * m], hbf[:, hh * 64:(hh + 1) * 64], Csg[:, hh, 0:m], start=False, stop=True)
                pbs = bank()
                S.mm(pbs[:, 0:512], Btok[0:m, g, :], V(xdtd.h[0:m, 8 * g:8 * g + 8, :].rearrange("p h d -> p (h d)"), [xdtd.base]))
                cdb = V(cdec.h[:, 8 * g:8 * g + 8].rearrange("p (h o) -> p h o", o=1).to_broadcast([128, 8, 64]), [cdec.base])
                S.tt("dve", V(htmp.h[:, :].rearrange("p (h d) -> p h d", d=64), [htmp.base]),
                     V(hs.h[:, g * 512:(g + 1) * 512].rearrange("p (h d) -> p h d", d=64), [hs.base]), cdb, ALU.mult)
                S.tt("dve", hs[:, g * 512:(g + 1) * 512], htmp[:, :], pbs[:, 0:512], ALU.add)
                for jj in range(4):
                    oc = 4 * g + jj
                    S.stt(y3[:, jj, 0:m], xbcT[:, oc, c0:c0 + m], pcol(l, "d_col", oc), yb[:, jj * m:(jj + 1) * m], ALU.mult, ALU.add)
                S.tt("dve", y3[:, :, 0:m], y3[:, :, 0:m], szT[:, 4 * g:4 * g + 4, c0:c0 + m], ALU.mult)
                S.act(ysq[:, :, 0:m], y3[:, :, 0:m], AF.Square)
                pbn = bank()
                for jj in range(4):
                    S.mm(pbn[:, 0:m], ones, ysq[:, jj, 0:m], start=(jj == 0), stop=(jj == 3))
                S.act(rs[:, 0:m], pbn[:, 0:m], AF.Sqrt, bias=cst_eps[:, 0:1], scale=1.0 / 512)
                S.recip(rs[:, 0:m], rs[:, 0:m])
                for jj in range(4):
                    oc = 4 * g + jj
                    S.stt(ossdT[:, oc, c0:c0 + m], y3[:, jj, 0:m], pcol(l, "ssd_norm", oc), rs[:, 0:m], ALU.mult, ALU.mult)
            if sg.store_state is not None:
                sg.store_state(l)

    def stage_F(l, n, msegs):
        AR.reset()
        macc = AR.get([128, 8, NT], F32)
        mergedT = AR.get([128, 8, NT], BF16)
        sig = [AR.get([128, NT], F32) for _ in range(2)]
        branches = (("wb_attn", 16, 64, lambda k: V(oattnT.h[0:64, k, 0:n], [oattnT.base])),
                    ("wb_pool", 8, 128, lambda k: opoolT[:, k, 0:n]),
                    ("wb_ssd", 16, 128, lambda k: ossdT[:, k, 0:n]))
        for b, (wn, kc, rows, rhs_fn) in enumerate(branches):
            for dh in range(4):
                gw = wload(wsrc("w_in", l, 0, 8, O_G + b * 1024 + dh * 256, 256), 8, 256)
                bw = wload(wsrc(wn, l, 0, kc, dh * 256, 256, rows=rows), kc, 256, rows=rows)
                for d2 in range(2):
                    dc = dh * 2 + d2
                    pg = bank()
                    for k in range(8):
                        S.mm(pg[:, 0:n], gw[:, k, d2 * 128:(d2 + 1) * 128], hT[:, k, 0:n], start=(k == 0), stop=(k == 7))
                    pbr = bank()
                    for k in range(kc):
                        S.mm(pbr[:, 0:n], bw[:, k, d2 * 128:(d2 + 1) * 128], rhs_fn(k), start=(k == 0), stop=(k == kc - 1))
                    sg_ = sig[dc % 2]
                    S.act(sg_[:, 0:n], pg[:, 0:n], AF.Sigmoid)
                    if b == 0:
                        S.tt("dve", macc[:, dc, 0:n], sg_[:, 0:n], pbr[:, 0:n], ALU.mult)
                    else:
                        S.tt("dve", sg_[:, 0:n], sg_[:, 0:n], pbr[:, 0:n], ALU.mult)
                        if b == 1:
                            S.tt("pool", macc[:, dc, 0:n], macc[:, dc, 0:n], sg_[:, 0:n], ALU.add)
                        else:
                            S.tt("dve", mergedT[:, dc, 0:n], macc[:, dc, 0:n], sg_[:, 0:n], ALU.add)
        for dh in range(2):
            ow = wload(wsrc("w_out", l, 0, 8, dh * 512, 512), 8, 512)
            for d4 in range(4):
                dc = dh * 4 + d4
                pb = bank()
                for k in range(8):
                    S.mm(pb[:, 0:n], ow[:, k, d4 * 128:(d4 + 1) * 128], mergedT[:, k, 0:n], start=(k == 0), stop=(k == 7))
                for (c0, sn, si) in msegs:
                    S.stt(xT[:, dc, c0:c0 + sn], pb[:, c0:c0 + sn], mod_gate(l, 0, dc, si), xT[:, dc, c0:c0 + sn], ALU.mult, ALU.add)

    def stage_G(l, n, msegs, csegs):
        rmsnorm_mod(l, 1, n, msegs)
        AR.reset()
        nsg = len(csegs)
        sn = csegs[0][1]
        actT = AR.get([128, 22, NT], BF16)
        stgc = AR.get([128, nsg, 2 + sn], F32)
        ca = [AR.get([128, NT], F32) for _ in range(2)]
        cg = AR.get([128, NT], F32)
        for j2 in range(11):
            aw = wload(wsrc("ffn_up", l, 0, 8, j2 * 256, 256), 8, 256)
            gwt = wload(wsrc("ffn_up", l, 0, 8, DFF + j2 * 256, 256), 8, 256)
            for o in range(2):
                j = j2 * 2 + o
                pa = bank()
                for k in range(8):
                    S.mm(pa[:, 0:n], aw[:, k, o * 128:(o + 1) * 128], hT[:, k, 0:n], start=(k == 0), stop=(k == 7))
                ca_ = ca[j % 2]
                conv_fm(ca_[:, 0:n], pa, n, csegs, lambda st: st.fconv_h[l], j, 2, "fconv_w", "fconv_b", l, stgc)
                pg = bank()
                for k in range(8):
                    S.mm(pg[:, 0:n], gwt[:, k, o * 128:(o + 1) * 128], hT[:, k, 0:n], start=(k == 0), stop=(k == 7))
                conv_fm(cg[:, 0:n], pg, n, csegs, lambda st: st.fconv_h[l], 22 + j, 2, "fconv_w", "fconv_b", l, stgc)
                S.act(cg[:, 0:n], cg[:, 0:n], AF.Silu)
                S.tt("dve", actT[:, j, 0:n], cg[:, 0:n], ca_[:, 0:n], ALU.mult)
        for dh in range(8):
            dw = wload(wsrc("ffn_down", l, 0, 22, dh * 128, 128), 22, 128)
            pb = bank()
            for k in range(22):
                S.mm(pb[:, 0:n], dw[:, k, :], actT[:, k, 0:n], start=(k == 0), stop=(k == 21))
            for (c0, sn_, si) in msegs:
                S.stt(xT[:, dh, c0:c0 + sn_], pb[:, c0:c0 + sn_], mod_gate(l, 1, dh, si), xT[:, dh, c0:c0 + sn_], ALU.mult, ALU.add)

    def load_x(src_rows, n, c0):
        S.dma_in("sp", xtok[0:n, :], src_rows)
        for hb in range(2):
            pb = bank()
            for jj in range(4):
                j = hb * 4 + jj
                S.tr(pb[:, jj * n:(jj + 1) * n], xtok[0:n, j * 128:(j + 1) * 128], ident[0:n, 0:n])
            S.copy("act", xT[:, hb * 4:hb * 4 + 4, c0:c0 + n], V(pb.h[:, 0:4 * n].rearrange("p (j t) -> p j t", j=4), [pb.base]))

    def store_x(dst_rows, n, c0):
        for hb in range(2):
            pb = bank()
            for jj in range(4):
                j = hb * 4 + jj
                S.tr(pb[0:n, jj * 128:(jj + 1) * 128], xT[:, j, c0:c0 + n], ident)
            S.copy("act", xtok[0:n, hb * 512:(hb + 1) * 512], pb[0:n, 0:512])
        S.dma_out("pool", dst_rows, xtok[0:n, :])

    def store_hs(hs, dst):
        AR.reset()
        hst = AR.get([128, 16, 128], F32)
        for q in range(4):
            pb = bank()
            for jj in range(4):
                j = q * 4 + jj
                S.tr(pb[:, jj * 128:(jj + 1) * 128], hs[:, j * 128:(j + 1) * 128], ident)
            S.copy("act", hst[:, q * 4:q * 4 + 4, :], V(pb.h[:, :].rearrange("p (j t) -> p j t", j=4), [pb.base]))
        S.dma_out("pool", dst.rearrange("(c p) n -> p c n", p=128), hst[:, :, :])

    def store_halos(st, l, dpool, dsconv, dfconv):
        halo_store(dpool, st.pool_h[l], 8, 15)
        halo_store(dsconv, st.sconv_h[l], 24, 3)
        halo_store(dfconv, st.fconv_h[l], 44, 2)

    for ch in range(NCH):
        segs = []
        for i in range(NT // 128):
            sg = Seg()
            sg.i, sg.n, sg.c0 = i, 128, i * 128
            sg.key0 = ch * NT + i * 128
            sg.topk = cfg.topk_p
            sg.state = pstate
            sg.rope = rope_p[sg.key0:sg.key0 + 128, :]
            sg.nk_out = (lambda l, k0=sg.key0: nk_p[l, k0:k0 + 128, :])
            sg.nv_out = (lambda l, k0=sg.key0: nv_p[l, k0:k0 + 128, :])
            sg.nki_out = (lambda l, k0=sg.key0: nki_p[l, k0:k0 + 128, :])
            sg.load_state = None
            sg.store_state = None
            segs.append(sg)
            load_x(xp[sg.key0:sg.key0 + 128, :], 128, sg.c0)
        msegs = [(0, NT, 0)]
        csegs = [(0, NT, pstate)]
        for l in range(L):
            for sg in segs:
                sg.kt, sg.kit, sg.va = pkt[l], pkit[l], pva[l]
            rmsnorm_mod(l, 0, NT, msegs)
            stage_B(l, segs)
            for sg in segs:
                stage_C(l, sg)
            stage_D(l, NT, csegs, ch == 0)
            stage_E(l, NT, csegs, segs)
            stage_F(l, NT, msegs)
            stage_G(l, NT, msegs, csegs)
        for sg in segs:
            store_x(y_p[sg.key0:sg.key0 + 128, :], 128, sg.c0)
    for l in range(L):
        store_hs(pstate.hs[l], nssd_p[l])
        store_halos(pstate, l, npool_p[l], nsconv_p[l], nfconv_p[l])

    NSC = NST
    segs = []
    for s in range(NS):
        sg = Seg()
        sg.i, sg.n, sg.c0 = s, TS, s * TS
        sg.key0 = PAST
        sg.topk = cfg.topk_s
        sg.state = sstates[s]
        sg.rope = rope_s[:, :]
        sg.nk_out = (lambda l, s=s: nk_s[l, s * TS:(s + 1) * TS, :])
        sg.nv_out = (lambda l, s=s: nv_s[l, s * TS:(s + 1) * TS, :])
        sg.nki_out = (lambda l, s=s: nki_s[l, s * TS:(s + 1) * TS, :])

        def _load(l, s=s):
            AR_ = AR
            for q in range(4):
                S.dma_in("sp", hstg[:, :, :], st_ssd[l, s, q * 512:(q + 1) * 512, :].rearrange("(c p) n -> p c n", p=128))
                pb = bank()
                for jj in range(4):
                    S.tr(pb[:, jj * 128:(jj + 1) * 128], hstg[:, jj, :], ident)
                S.copy("act", hs_shared[:, q * 512:(q + 1) * 512], pb[:, :])

        def _store(l, s=s):
            for q in range(4):
                pb = bank()
                for jj in range(4):
                    j = q * 4 + jj
                    S.tr(pb[:, jj * 128:(jj + 1) * 128], hs_shared[:, j * 128:(j + 1) * 128], ident)
                S.copy("act", hstg[:, :, :], V(pb.h[:, :].rearrange("p (j t) -> p j t", j=4), [pb.base]))
                S.dma_out("pool", nssd_s[l, s, q * 512:(q + 1) * 512, :].rearrange("(c p) n -> p c n", p=128), hstg[:, :, :])
        sg.load_state = _load
        sg.store_state = _store
        segs.append(sg)
    hstg = AliasT(xtok, xtok.h[:, 0:512].rearrange("p (j t) -> p j t", j=4))
    load_x(xs[:, :], NST, 0)
    msegs = [(s * TS, TS, 1 + s) for s in range(NS)]
    csegs = [(s * TS, TS, sstates[s]) for s in range(NS)]
    kpg = AliasT(relu_b[0], relu_b[0].h[:, :].bitcast(BF16)[:, 0:384])
    vpg = AliasT(relu_b[1], relu_b[1].h[:, :].bitcast(BF16)[:, 0:256])
    kipg = AliasT(relu_b[1], relu_b[1].h[:, :].bitcast(BF16)[:, 256:448])
    for l in range(L):
        for s in range(NS):
            for j in range(NPG):
                for h_ in range(PCS):
                    ixh = idxall[:, h_, s * NPG + j:s * NPG + j + 1]
                    S.gather(kpg[:, 64:64 + 256], ck[l * PCS + h_], ixh, PR - 1)
                    S.gather(vpg[:, :], cv[l * PCS + h_], ixh, PR - 1)
                S.gather(kipg[:, 64:128], cki[l], idxall[:, 0, s * NPG + j:s * NPG + j + 1], cfg.NPOOL * 128 - 1)
                pb = bank()
                pbv = bfv(pb)
                for g in range(4):
                    S.tr(pbv[:, g * 128:(g + 1) * 128], kpg[:, 64 + 64 * g:64 + 64 * g + 128], identb)
                kt_ = ktile[j % 3]
                S.copy("act", V(kt_.h[0:64, :, :], [kt_.base]), V(pbv.ap0[0:64, 0:512].rearrange("p (g t) -> p g t", g=4), [pb.base]))
                S.dma_out("sp", V(skt[l][s].ap[:, :, j * 128:(j + 1) * 128], skt[l][s].trks), V(kt_.h[0:64, :, :], [kt_.base]))
                vt_ = vtile[j % 3]
                S.memset("pool", vt_[:, :, 64:65], 1.0)
                S.copy("dve", vt_[:, :, 0:64], V(vpg.h[:, :].rearrange("p (g d) -> p g d", g=4), [vpg.base]))
                S.dma_out("sp", V(sva[l][s].ap[j * 128:(j + 1) * 128, :, :], sva[l][s].trks), vt_[:, :, :])
                pb2 = bank()
                pbv2 = bfv(pb2)
                S.tr(pbv2[:, 0:128], kipg[:, 64:64 + 128], identb)
                ki_ = pexp[j % 2]
                S.copy("act", ki_[0:64, 0:128], pbv2[0:64, 0:128])
                S.dma_out("sp", V(skit[l][s].ap[:, j * 128:(j + 1) * 128], skit[l][s].trks), ki_[0:64, 0:128])
        for s, sg in enumerate(segs):
            sg.kt, sg.kit, sg.va = skt[l][s], skit[l][s], sva[l][s]
        rmsnorm_mod(l, 0, NSC, msegs)
        stage_B(l, segs)
        for sg in segs:
            stage_C(l, sg)
        stage_D(l, NSC, csegs, False)
        stage_E(l, NSC, csegs, segs)
        stage_F(l, NSC, msegs)
        stage_G(l, NSC, msegs, csegs)
        for s in range(NS):
            store_halos(sstates[s], l, npool_s[l, s], nsconv_s[l, s], nfconv_s[l, s])
    store_x(y_s[:, :], NST, 0)

    nops = S.emit()
    return nc, nops


def _consts(cfg):
    c = np.zeros((128, 4, 128), np.float32)
    tri = np.triu(np.ones((128, 128), np.float32))
    c[:, 0, :] = np.eye(128, dtype=np.float32)
    c[:, 1, :] = tri
    c[:, 2, :] = (tri - 1.0) * 1.0e4
    c[:, 3, :] = (tri.T - 1.0) * 1.0e30
    half = 32
    freqs = (np.float32(10000.0) ** (-np.arange(half, dtype=np.float32) / np.float32(half))).astype(np.float32)

    def rope(pos):
        ang = pos.astype(np.float32)[:, None] * freqs[None, :]
        return np.concatenate([np.cos(ang), np.sin(ang)], axis=1).astype(np.float32)
    rp = rope(np.arange(cfg.T))
    rs = rope(cfg.PAST + np.arange(cfg.TS))
    cf = np.ones((128, 4, 16), np.float32)
    for gi, w in enumerate((2, 4, 8, 16)):
        for t in range(15):
            cf[:, gi, t] = float(w) / float(min(w, t + 1))
    return c, rp, rs, cf


def _ptab(inp, l):
    def fm(v):
        return np.ascontiguousarray(v.reshape(-1, 128).T)
    cols = [fm(inp["norm1"][l]), fm(inp["norm2"][l]), fm(inp["pool_scale"][l])]
    cols += [fm(inp["ssd_conv_w"][l][j]) for j in range(4)]
    cols.append(fm(inp["ssd_conv_b"][l]))
    cols.append(fm(inp["ssd_norm"][l]))
    cols += [fm(inp["ffn_conv_w"][l][j]) for j in range(3)]
    cols.append(fm(inp["ffn_conv_b"][l]))
    cols.append(fm(np.repeat(inp["ssd_d"][l], 64)))
    return np.concatenate(cols, axis=1).astype(np.float32)


def run_cfg(cfg, inp, n_cores=8, n_prompt=4):
    nc, nops = build(cfg)
    inp = {k: np.asarray(v) for k, v in inp.items()}
    c, rp, rs, cf = _consts(cfg)
    ptab = np.stack([_ptab(inp, l) for l in range(L)])
    frow = np.stack([np.concatenate([inp["q_norm"][l], inp["k_norm"][l], inp["ssd_dt_bias"][l], inp["ssd_a_log"][l]])[None, :]
                     for l in range(L)]).astype(np.float32)
    NS = cfg.NS
    shared = {
        "b_ada": inp["b_ada"].reshape(L, 1, -1),
        "ptab": ptab, "frow": frow, "consts": c, "rope_p": rp, "rope_s": rs, "cntfix": cf,
    }
    in_maps = []
    for core in range(n_cores):
        b = core % n_prompt
        sl = slice(core * NS, (core + 1) * NS)
        m = dict(shared)
        for nm_, key_, cols_, pcs_ in (("ck_sh", "cache_k", 256, 2), ("cv_sh", "cache_v", 256, 2), ("cki_sh", "cache_kidx", 64, 1)):
            a_ = inp[key_].reshape(L * pcs_, n_cores, -1, cols_)[:, core]
            m[nm_] = a_.reshape(-1, cols_)
        for nm_, key_ in (("w_in", "w_in"), ("pool_w", "pool_w"), ("wb_attn", "w_branch_attn"), ("wb_pool", "w_branch_pool"),
                          ("wb_ssd", "w_branch_ssd"), ("w_out", "w_out"), ("ffn_up", "ffn_up"), ("ffn_down", "ffn_down"),
                          ("w_ada", "w_ada")):
            a_ = inp[key_].reshape(-1, inp[key_].shape[-1])
            rs_ = a_.shape[0] // n_cores
            m[nm_ + "_sh"] = a_[core * rs_:(core + 1) * rs_]
        m["xp"] = inp["x_prompt"][b]
        m["xs"] = inp["x_sample"][sl].reshape(NS * cfg.TS, D)
        m["call"] = np.concatenate([inp["c_prompt"][b:b + 1], inp["c_sample"][sl]], axis=0)
        m["pt"] = inp["page_table"][sl].reshape(1, -1).astype(np.int32)
        m["st_pool"] = inp["state_pool"][:, sl]
        m["st_sconv"] = inp["state_ssd_conv"][:, sl]
        m["st_ssd"] = inp["state_ssd"][:, sl].reshape(L, NS, DSSD, SN)
        m["st_fconv"] = inp["state_ffn_conv"][:, sl]
        in_maps.append({k: np.ascontiguousarray(v) for k, v in m.items()})
    res = run_bass_kernel_spmd(nc, in_maps, core_ids=list(range(n_cores))).results
    T, TS = cfg.T, cfg.TS
    P = range(n_prompt)

    def stackp(name, shp):
        return np.stack([res[b][name].reshape(shp) for b in P], axis=0)

    def stackp_l(name, shp):
        return np.stack([res[b][name].reshape((L,) + shp) for b in P], axis=1)

    def cats_l(name, shp):
        return np.concatenate([res[c_][name].reshape((L, NS) + shp) for c_ in range(n_cores)], axis=1)

    outs = (
        stackp("y_p", (T, D)),
        np.concatenate([res[c_]["y_s"].reshape(NS, TS, D) for c_ in range(n_cores)], axis=0),
        stackp_l("nk_p", (T, NKV, HD)), stackp_l("nv_p", (T, NKV, HD)), stackp_l("nki_p", (T, ID)),
        stackp_l("npool_p", (15, D)), stackp_l("nsconv_p", (3, DCONV)), stackp_l("nssd_p", (SH, SP, SN)),
        stackp_l("nfconv_p", (2, 2 * DFF)),
        cats_l("nk_s", (TS, NKV, HD)), cats_l("nv_s", (TS, NKV, HD)), cats_l("nki_s", (TS, ID)),
        cats_l("npool_s", (15, D)), cats_l("nsconv_s", (3, DCONV)), cats_l("nssd_s", (SH, SP, SN)),
        cats_l("nfconv_s", (2, 2 * DFF)),
    )
    return tuple(np.ascontiguousarray(o, dtype=np.float32) for o in outs)


def kernel(**inputs):
    T = int(np.asarray(inputs["x_prompt"]).shape[1])
    npg = int(np.asarray(inputs["page_table"]).shape[1])
    npool = int(np.asarray(inputs["cache_k"]).shape[1])
    ts = int(np.asarray(inputs["x_sample"]).shape[1])
    cfg = Cfg(T=T, NPG=npg, NPOOL=npool, NS=4, TS=ts, topk_p=min(256, T // 4), topk_s=min(256, (npg * 128 + ts) // 4))
    return run_cfg(cfg, inputs)
```

```python
import numpy as np
import concourse.bass as bass
import concourse.mybir as mybir
from concourse.bass_utils import run_bass_kernel_spmd

F32 = mybir.dt.float32
BF16 = mybir.dt.bfloat16
I32 = mybir.dt.int32
AF = mybir.ActivationFunctionType
ALU = mybir.AluOpType
AX = mybir.AxisListType

D = 1024
L = 2
NH, NKV, HD = 16, 4, 64
IH, ID = 8, 64
DSSD, SH, SP, SG, SN = 2048, 32, 64, 4, 128
DCONV = 3072
DFF = 2816
DIN = 11368
O_Q, O_K, O_V, O_QI, O_KI, O_WI, O_PU, O_Z, O_XBC, O_DT, O_G = 0, 1024, 1280, 1536, 2048, 2112, 2120, 3144, 5192, 8264, 8296
EPS = 1e-6
NEG = -1.0e30
NBIS = 20
FOLD_WAIT = True


class Op:
    __slots__ = ("eng", "fn", "deps", "dma", "dsem", "dval", "dprev", "inc", "idx", "cc")


class Trk:
    __slots__ = ("w", "rd", "prev")

    def __init__(self, prev=None):
        self.w = None
        self.rd = []
        self.prev = prev or []


class V:
    __slots__ = ("ap", "trks")

    def __init__(self, ap, trks):
        self.ap = ap
        self.trks = trks

    def __getitem__(self, idx):
        return V(self.ap[idx], self.trks)


class Tile:
    def __init__(self, sch, name, shape, dtype, psum=False):
        nc = sch.nc
        self.h = nc.alloc_psum_tensor(name, list(shape), dtype) if psum else nc.alloc_sbuf_tensor(name, list(shape), dtype)
        self.base = Trk()
        self.parts = {}
        self.shape = shape

    def __getitem__(self, idx):
        return V(self.h[idx], [self.base])

    def p(self, key):
        if key not in self.parts:
            self.parts[key] = Trk()
        return _Part(self, self.parts[key])

    def all(self):
        return _All(self)


class Sub:
    def __init__(self, ap, prev=None):
        self.h = ap
        self.base = Trk(prev)

    def __getitem__(self, idx):
        return V(self.h[idx], [self.base])

    def ops(self):
        r = list(self.base.rd) + list(self.base.prev)
        if self.base.w is not None:
            r.append(self.base.w)
        return r


class AliasT:
    def __init__(self, tile, ap):
        self.h = ap
        self.base = tile.base

    def __getitem__(self, idx):
        return V(self.h[idx], [self.base])


class Arena:
    def __init__(self, sch, nbytes, name):
        self.t = sch.tile([128, nbytes // 4], F32, name)
        self.nbytes = nbytes
        self.subs = []
        self.off = 0
        self.prev = []

    def reset(self):
        prev = []
        seen = set()
        for s_ in self.subs:
            for o in s_.ops():
                if id(o) not in seen:
                    seen.add(id(o))
                    prev.append(o)
        for o in self.prev:
            if id(o) not in seen:
                seen.add(id(o))
                prev.append(o)
        self.prev = prev
        self.subs = []
        self.off = 0

    def get(self, shape, dtype):
        esz = 4 if dtype in (F32, I32) else 2
        n = 1
        for s_ in shape[1:]:
            n *= s_
        nb = (n * esz + 31) // 32 * 32
        assert self.off + nb <= self.nbytes, ("arena overflow", self.off, nb, self.nbytes)
        ap = self.t.h[:, self.off // 4:(self.off + nb) // 4]
        if esz == 2:
            ap = ap.bitcast(dtype)
        elif dtype != F32:
            ap = ap.bitcast(dtype)
        ap = ap[0:shape[0], 0:n]
        if len(shape) > 2:
            names = " ".join("a%d" % i for i in range(len(shape) - 1))
            kw = {"a%d" % i: shape[i + 1] for i in range(len(shape) - 1)}
            ap = ap.rearrange("p (%s) -> p %s" % (names, names), **kw)
        self.off += nb
        sub = Sub(ap, list(self.prev))
        self.subs.append(sub)
        return sub


class _Part:
    def __init__(self, t, trk):
        self.t, self.trk = t, trk

    def __getitem__(self, idx):
        return V(self.t.h[idx], [self.trk])


class _All:
    def __init__(self, t):
        self.t = t

    def __getitem__(self, idx):
        return V(self.t.h[idx], [self.t.base] + list(self.t.parts.values()))


ENGS = ("pe", "dve", "act", "pool", "sp")
NDS = 12


class Sched:
    def __init__(self, nc):
        self.nc = nc
        self.ops = []
        self.e = {"pe": nc.tensor, "dve": nc.vector, "act": nc.scalar, "pool": nc.gpsimd, "sp": nc.sync}
        self.sem = {k: nc.alloc_semaphore("s_" + k) for k in ENGS}
        self.dsems = {q: [nc.alloc_semaphore("d_%s%d" % (q, i)) for i in range(NDS)] for q in ("sp", "pool", "act")}
        self.dcount = {"sp": 0, "pool": 0, "act": 0}
        self.dlast = {"sp": {}, "pool": {}, "act": {}}
        self.nt = 0

    def tile(self, shape, dtype, name=None, psum=False):
        self.nt += 1
        return Tile(self, (name or "t") + "_%d" % self.nt, shape, dtype, psum)

    def add(self, eng, fn, reads=(), writes=(), dma=False, cc=False):
        op = Op()
        op.eng, op.fn, op.dma = eng, fn, dma
        op.cc = cc
        op.dprev = None
        op.inc = False
        op.idx = 0
        deps = {}
        for v in reads:
            for t in v.trks:
                if t.w is not None:
                    deps[id(t.w)] = (t.w, "raw")
        for v in writes:
            for t in v.trks:
                if t.w is not None and id(t.w) not in deps:
                    deps[id(t.w)] = (t.w, "waw")
                for r in t.rd:
                    if id(r) not in deps:
                        deps[id(r)] = (r, "war")
                for r in t.prev:
                    if id(r) not in deps:
                        deps[id(r)] = (r, "waw")
                t.prev = []
        fl = []
        for d, kind in deps.values():
            if d is op:
                continue
            if (not d.dma) and (not dma) and d.eng == eng:
                if eng == "pe":
                    continue
                if kind == "war":
                    continue
            fl.append(d)
            if not d.dma:
                d.inc = True
        op.deps = fl
        if cc:
            op.dsem = self.nc.alloc_semaphore("cc%d" % len(self.ops))
            op.dval = 1
        elif dma:
            i = self.dcount[eng]
            self.dcount[eng] += 1
            op.dsem = self.dsems[eng][i % NDS]
            op.dval = 16 * (i // NDS + 1)
            op.dprev = self.dlast[eng].get(i % NDS)
            self.dlast[eng][i % NDS] = op
        for v in writes:
            for t in v.trks:
                t.w = op
                t.rd = []
        for v in reads:
            for t in v.trks:
                if t.w is op:
                    continue
                if not dma:
                    t.rd = [r for r in t.rd if r.dma or r.eng != eng]
                t.rd.append(op)
        self.ops.append(op)
        return op

    def emit(self):
        cnt = {k: 0 for k in ENGS}
        for op in self.ops:
            if op.inc:
                cnt[op.eng] += 1
                op.idx = cnt[op.eng]
        waited = {k: {} for k in ENGS}
        for op in self.ops:
            e = self.e[op.eng]
            w = waited[op.eng]
            need = {}
            for d in op.deps:
                if d.dma:
                    key, val = d.dsem, d.dval
                else:
                    key, val = self.sem[d.eng], d.idx
                if need.get(key, 0) < val:
                    need[key] = val
            if op.dma and op.dprev is not None:
                key, val = op.dprev.dsem, op.dprev.dval
                if need.get(key, 0) < val:
                    need[key] = val
            todo = []
            for key, val in need.items():
                if w.get(key, 0) < val:
                    todo.append((key, val))
                    w[key] = val
            fold = None
            if todo and FOLD_WAIT and not op.dma:
                fold = todo.pop()
            for key, val in todo:
                e.wait_ge(key, val)
            ins = op.fn(e)
            if fold is not None:
                ins._wait_ge(fold[0], fold[1])
            if op.cc:
                ins.then_inc(op.dsem)
            elif op.dma:
                ins.then_inc(op.dsem, 16)
            elif op.inc:
                ins.then_inc(self.sem[op.eng], 1)
        sp = self.e["sp"]
        for q in ("sp", "pool", "act"):
            for op in self.dlast[q].values():
                sp.wait_ge(op.dsem, op.dval)
        return len(self.ops)

    def mm(self, out, lhsT, rhs, start=True, stop=True):
        return self.add("pe", lambda e: e.matmul(out.ap, lhsT=lhsT.ap, rhs=rhs.ap, start=start, stop=stop), [lhsT, rhs], [out])

    def tr(self, out, in_, ident):
        return self.add("pe", lambda e: e.transpose(out.ap, in_.ap, ident.ap), [in_, ident], [out])

    def act(self, out, in_, func, bias=None, scale=None, accum=None, eng="act"):
        rd = [in_]
        kw = {}
        if bias is not None:
            if isinstance(bias, V):
                rd.append(bias)
                kw["bias"] = bias.ap
            else:
                kw["bias"] = bias
        if scale is not None:
            if isinstance(scale, V):
                rd.append(scale)
                kw["scale"] = scale.ap
            else:
                kw["scale"] = scale
        wr = [out]
        if accum is not None:
            wr.append(accum)
            kw["accum_out"] = accum.ap
        return self.add("act", lambda e: e.activation(out=out.ap, in_=in_.ap, func=func, **kw), rd, wr)

    def tt(self, eng, out, a, b, op):
        return self.add(eng, lambda e: e.tensor_tensor(out=out.ap, in0=a.ap, in1=b.ap, op=op), [a, b], [out])

    def ts(self, eng, out, a, s1, s2=None, op0=ALU.mult, op1=None, accum=None):
        rd = [a]
        s1a = s1.ap if isinstance(s1, V) else s1
        s2a = s2.ap if isinstance(s2, V) else s2
        if isinstance(s1, V):
            rd.append(s1)
        if isinstance(s2, V):
            rd.append(s2)
        wr = [out]
        kw = {}
        if op1 is not None:
            kw["op1"] = op1
        if accum is not None:
            wr.append(accum)
            kw["accum_out"] = accum.ap
        return self.add(eng, lambda e: e.tensor_scalar(out=out.ap, in0=a.ap, scalar1=s1a, scalar2=s2a, op0=op0, **kw), rd, wr)

    def stt(self, out, a, s, b, op0, op1):
        rd = [a, b]
        sa = s.ap if isinstance(s, V) else s
        if isinstance(s, V):
            rd.append(s)
        return self.add("dve", lambda e: e.scalar_tensor_tensor(out=out.ap, in0=a.ap, scalar=sa, in1=b.ap, op0=op0, op1=op1), rd, [out])

    def copy(self, eng, out, in_):
        if eng == "act":
            return self.add("act", lambda e: e.copy(out=out.ap, in_=in_.ap), [in_], [out])
        return self.add(eng, lambda e: e.tensor_copy(out=out.ap, in_=in_.ap), [in_], [out])

    def recip(self, out, in_):
        return self.add("dve", lambda e: e.reciprocal(out=out.ap, in_=in_.ap), [in_], [out])

    def memset(self, eng, out, val):
        return self.add(eng, lambda e: e.memset(out.ap, val), [], [out])

    def reduce(self, out, in_, op, axis=AX.X):
        return self.add("dve", lambda e: e.tensor_reduce(out=out.ap, in_=in_.ap, axis=axis, op=op), [in_], [out])

    def dma_in(self, q, out, src, **kw):
        if isinstance(src, V):
            return self.add(q, lambda e: e.dma_start(out=out.ap, in_=src.ap, **kw), [src], [out], dma=True)
        return self.add(q, lambda e: e.dma_start(out=out.ap, in_=src, **kw), [], [out], dma=True)

    def dma_out(self, q, dst, in_, **kw):
        if isinstance(dst, V):
            return self.add(q, lambda e: e.dma_start(out=dst.ap, in_=in_.ap, **kw), [in_], [dst], dma=True)
        return self.add(q, lambda e: e.dma_start(out=dst, in_=in_.ap, **kw), [in_], [], dma=True)

    def gather(self, out, src, idx, bound):
        regs = self.__dict__.setdefault("_bregs", {})

        def fn(e):
            if bound not in regs:
                regs[bound] = e.to_reg(bound)
            return e.indirect_dma_start(
                out=out.ap, out_offset=None, in_=src.ap,
                in_offset=bass.IndirectOffsetOnAxis(ap=idx.ap, axis=0), bounds_check=regs[bound], oob_is_err=False)
        return self.add("pool", fn, [idx, src], [out], dma=True)

    def allgather(self, out, in_, ncores):
        return self.add("pool", lambda e: e.collective_compute(
            "AllGather", ALU.bypass, replica_groups=[list(range(ncores))],
            ins=[in_.ap.opt()], outs=[out.ap.opt()]), [in_], [out], dma=True, cc=True)


class Cfg:
    def __init__(self, T=4096, NPG=64, NPOOL=2560, NS=4, TS=4, topk_p=256, topk_s=256, NT=256, NCORES=8):
        self.T, self.NPG, self.NPOOL, self.NS, self.TS = T, NPG, NPOOL, NS, TS
        self.topk_p, self.topk_s = topk_p, topk_s
        self.NT = min(NT, T)
        self.NCH = T // self.NT
        self.PAST = NPG * 128
        self.NCORES = NCORES


PCOLS = {}
_off = 0
for _n, _c in (("norm1", 8), ("norm2", 8), ("pool_scale", 8), ("sconv_w", 4 * 24), ("sconv_b", 24), ("ssd_norm", 16),
               ("fconv_w", 3 * 44), ("fconv_b", 44), ("d_col", 16)):
    PCOLS[_n] = (_off, _c)
    _off += _c
NPCOL = _off


class Seg:
    pass


WSH = {"w_in": [L, D, DIN], "pool_w": [L, 4, 256, 256], "wb_attn": [L, D, D], "wb_pool": [L, D, D],
       "wb_ssd": [L, DSSD, D], "w_out": [L, D, D], "ffn_up": [L, D, 2 * DFF], "ffn_down": [L, DFF, D]}


def build(cfg):
    nc = bass.Bass("TRN2", target_bir_lowering=False)
    S = Sched(nc)
    T, NT, NCH, NS, TS, NPG = cfg.T, cfg.NT, cfg.NCH, cfg.NS, cfg.TS, cfg.NPG
    NSQ = 1 + NS
    NST = NS * TS
    PAST = cfg.PAST
    NKS = PAST + 128
    NKMAX = max(T, NKS)

    def din(name, shape, dt=F32):
        return nc.dram_tensor(name, list(shape), dt, kind="ExternalInput").ap()

    def dout(name, shape, dt=F32):
        return nc.dram_tensor(name, list(shape), dt, kind="ExternalOutput").ap()

    def dscr(name, shape, dt):
        return nc.dram_tensor(name, list(shape), dt, kind="Internal").ap()

    xp = din("xp", [T, D])
    xs = din("xs", [NST, D])
    call = din("call", [NSQ, D])
    NCO = cfg.NCORES

    GATH = []

    def gathered(name, rows, cols, nparts=1, dt=BF16):
        rs = rows // NCO
        sh = din(name + "_sh", [nparts * rs, cols])
        bnc = V(dscr(name + "_bin", [nparts * rs, cols], dt), [Trk()])
        fulls = [V(dscr(name + "_full%d" % q, [rows, cols], dt), [Trk()]) for q in range(nparts)]
        GATH.append((sh, bnc, fulls, rs, cols, nparts, dt))
        return fulls

    wbf = {}
    for nm_, shp_ in WSH.items():
        rows_ = 1
        for s_ in shp_[:-1]:
            rows_ *= s_
        f_ = gathered(nm_, rows_, shp_[-1])[0]
        if len(shp_) == 3:
            wbf[nm_] = V(f_.ap.rearrange("(l r) c -> l r c", l=L), f_.trks)
        else:
            wbf[nm_] = V(f_.ap.rearrange("(l g r) c -> l g r c", l=L, g=4), f_.trks)
    w_ada_full = gathered("w_ada", L * D, 6 * D, 1, F32)[0]
    w_ada = V(w_ada_full.ap.rearrange("(l k) c -> l k c", l=L), w_ada_full.trks)
    PCS = 2
    PR = cfg.NPOOL * 128 // PCS
    ck = gathered("ck", PR, 256, L * PCS)
    cv = gathered("cv", PR, 256, L * PCS)
    cki = gathered("cki", cfg.NPOOL * 128, 64, L)
    pt = din("pt", [1, NS * NPG], I32)
    st_pool = din("st_pool", [L, NS, 15, D])
    st_sconv = din("st_sconv", [L, NS, 3, DCONV])
    st_ssd = din("st_ssd", [L, NS, DSSD, SN])
    st_fconv = din("st_fconv", [L, NS, 2, 2 * DFF])
    b_ada = din("b_ada", [L, 1, 6 * D])
    ptab = din("ptab", [L, 128, NPCOL])
    frow = din("frow", [L, 1, 192])
    consts = din("consts", [128, 4, 128])
    rope_p = din("rope_p", [T, 64])
    rope_s = din("rope_s", [TS, 64])
    cntfix = din("cntfix", [128, 4, 16])

    y_p = dout("y_p", [T, D])
    y_s = dout("y_s", [NST, D])
    nk_p = dout("nk_p", [L, T, 256])
    nv_p = dout("nv_p", [L, T, 256])
    nki_p = dout("nki_p", [L, T, 64])
    npool_p = dout("npool_p", [L, 15, D])
    nsconv_p = dout("nsconv_p", [L, 3, DCONV])
    nssd_p = dout("nssd_p", [L, DSSD, SN])
    nfconv_p = dout("nfconv_p", [L, 2, 2 * DFF])
    nk_s = dout("nk_s", [L, NST, 256])
    nv_s = dout("nv_s", [L, NST, 256])
    nki_s = dout("nki_s", [L, NST, 64])
    npool_s = dout("npool_s", [L, NS, 15, D])
    nsconv_s = dout("nsconv_s", [L, NS, 3, DCONV])
    nssd_s = dout("nssd_s", [L, NS, DSSD, SN])
    nfconv_s = dout("nfconv_s", [L, NS, 2, 2 * DFF])

    pkt = [V(dscr("pkt%d" % l, [64, 4, T], BF16), [Trk()]) for l in range(L)]
    pkit = [V(dscr("pkit%d" % l, [64, T], BF16), [Trk()]) for l in range(L)]
    pva = [V(dscr("pva%d" % l, [T, 4, 65], BF16), [Trk()]) for l in range(L)]
    skt = [[V(dscr("skt%d_%d" % (l, s), [64, 4, NKS], BF16), [Trk()]) for s in range(NS)] for l in range(L)]
    skit = [[V(dscr("skit%d_%d" % (l, s), [64, NKS], BF16), [Trk()]) for s in range(NS)] for l in range(L)]
    sva = [[V(dscr("sva%d_%d" % (l, s), [NKS, 4, 65], BF16), [Trk()]) for s in range(NS)] for l in range(L)]

    AR = Arena(S, 57344, "arena")

    cst = S.tile([128, 4, 128], F32, "cst")
    S.dma_in("sp", cst[:], consts)
    ident = cst[:, 0, :]
    tri = cst[:, 1, :]
    negsl = cst[:, 2, :]
    negqk = cst[:, 3, :]
    identb_t = S.tile([128, 128], BF16, "identb")
    S.copy("dve", identb_t[:], ident)
    identb = identb_t[:]
    ones_t = S.tile([128, 128], F32, "ones")
    S.memset("dve", ones_t[:], 1.0)
    ones = ones_t[:]
    cst_eps = S.tile([128, 1], F32, "eps")
    S.memset("dve", cst_eps[:], EPS)
    cfix = S.tile([128, 4, 16], F32, "cfix")
    S.dma_in("sp", cfix[:], cntfix)
    pw2 = S.tile([128, NBIS + 2], F32, "pw2")
    for k in range(NBIS + 2):
        S.memset("pool", pw2[:, k:k + 1], 2.0 ** (-(k + 1)))
    ptb = S.tile([128, L, NPCOL], F32, "ptb")
    frw = S.tile([128, L, 192], F32, "frw")
    Abc = S.tile([128, L, 32], F32, "Abc")
    for l in range(L):
        S.dma_in("sp", ptb[:, l, :], ptab[l])
        S.dma_in("sp", frw[:, l, :], frow[l].to_broadcast([128, 192]))
        S.act(Abc[:, l, :], frw[:, l, 160:192], AF.Exp)
        S.ts("dve", Abc[:, l, :], Abc[:, l, :], -1.0)

    def pcol(l, name, j=0, n=1):
        o, c = PCOLS[name]
        return ptb[:, l, o + j:o + j + n]

    ptbc = S.tile([128, NS * NPG], I32, "ptbc")
    S.dma_in("sp", ptbc[:], pt.to_broadcast([128, NS * NPG]))
    iot = S.tile([128, 1], I32, "iot")
    S.add("pool", lambda e: e.iota(iot.h[:], pattern=[[0, 1]], base=0, channel_multiplier=1), [], [iot[:]])
    idxall = S.tile([128, PCS, NS * NPG], I32, "idxall")
    S.ts("dve", idxall[:, 0, :], ptbc[:], 128, iot[:, 0:1], op0=ALU.mult, op1=ALU.add)
    for h_ in range(1, PCS):
        S.ts("dve", idxall[:, h_, :], idxall[:, 0, :], -float(h_ * PR), None, op0=ALU.add)

    CV = 8192
    AR.reset()
    cvb = [AR.get([128, CV], BF16) for _ in range(2)]
    ci = 0
    for (sh, bnc, fulls, rs, cols, nparts, dt) in GATH:
        tot = nparts * rs * cols
        f1 = sh.rearrange("r c -> (r c)")
        f2 = bnc.ap.rearrange("r c -> (r c)")
        if dt == BF16:
            per = tot // 128
            src = f1.rearrange("(p f) -> p f", p=128)
            dst = f2.rearrange("(p f) -> p f", p=128)
            for f0 in range(0, per, CV):
                fn_ = min(CV, per - f0)
                bb = cvb[ci % 2]
                ci += 1
                S.dma_in("pool", bb[:, 0:fn_], src[:, f0:f0 + fn_])
                S.dma_out("sp", V(dst[:, f0:f0 + fn_], bnc.trks), bb[:, 0:fn_])
        else:
            ncp = 8
            per = tot // ncp
            for i in range(ncp):
                src_ = f1[i * per:(i + 1) * per].rearrange("(p f) -> p f", p=128)
                dst_ = f2[i * per:(i + 1) * per].rearrange("(p f) -> p f", p=128)
                S.add("pool", (lambda e, s_=src_, d_=dst_: e.dma_start(out=d_, in_=s_)), [], [bnc], dma=True)
        for q in range(nparts):
            S.allgather(fulls[q], V(bnc.ap[q * rs:(q + 1) * rs, :], bnc.trks), NCO)

    PS = [S.tile([128, 512], F32, "ps%d" % i, psum=True) for i in range(8)]
    psb = [0]

    def bank():
        b = PS[psb[0] % 4]
        psb[0] += 1
        return b

    def bfv(b):
        return Sub_shared(b)

    class Sub_shared:
        def __init__(self, b):
            self.b = b
            self.ap0 = b.h[:, :].bitcast(BF16)

        def __getitem__(self, idx):
            return V(self.ap0[idx], [self.b.base])

    WB = [S.tile([128, 4096], BF16, "wb%d" % i) for i in range(3)]
    wbi = [0]
    first_w = [True]

    def wload(src3, kc, ncol, rows=128):
        b = WB[wbi[0] % 3]
        wbi[0] += 1
        v = V(b.h[0:rows, 0:kc * ncol].rearrange("p (k c) -> p k c", k=kc), [b.base])
        S.add("sp", lambda e: e.dma_start(out=v.ap, in_=src3.ap), [src3], [v], dma=True)
        return v

    def wsrc(name, l, r0, kc, c0, ncol, rows=128):
        a = wbf[name].ap[l]
        return V(a[r0:r0 + kc * rows, c0:c0 + ncol].rearrange("(k p) c -> p k c", p=rows), wbf[name].trks)

    modT = S.tile([128, L, 48, NSQ], F32, "modT")
    amod = S.tile([128, L, 2, 8, NSQ], F32, "amod")
    AR.reset()
    adaw = [AR.get([128, 8, 512], F32) for _ in range(2)]
    ct = AR.get([NSQ, D], F32)
    cs_ = AR.get([NSQ, D], F32)
    cT = AR.get([128, 8, NSQ], F32)
    modtok = AR.get([NSQ, 512], F32)
    bada = AR.get([NSQ, 512], F32)
    S.dma_in("sp", ct[:], call)
    S.act(cs_[:], ct[:], AF.Silu)
    for kc in range(8):
        pb = bank()
        S.tr(pb[:, 0:NSQ], cs_[:, kc * 128:(kc + 1) * 128], ident[0:NSQ, 0:NSQ])
        S.copy("dve", cT[:, kc, :], pb[:, 0:NSQ])
    ai = 0
    for l in range(L):
        for cb in range(12):
            wt = adaw[ai % 2]
            ai += 1
            S.dma_in("sp", wt[:], V(w_ada.ap[l][:, cb * 512:(cb + 1) * 512].rearrange("(k p) c -> p k c", p=128), w_ada.trks))
            S.dma_in("sp", bada[:], b_ada[l][:, cb * 512:(cb + 1) * 512].to_broadcast([NSQ, 512]))
            pb = bank()
            for kc in range(8):
                S.mm(pb[0:NSQ, :], cT[:, kc, :], wt[:, kc, :], start=(kc == 0), stop=(kc == 7))
            S.tt("dve", modtok[:], pb[0:NSQ, :], bada[:], ALU.add)
            pb2 = bank()
            for j in range(4):
                S.tr(pb2[:, j * NSQ:(j + 1) * NSQ], modtok[:, j * 128:(j + 1) * 128], ident[0:NSQ, 0:NSQ])
            S.copy("act", modT[:, l, cb * 4:cb * 4 + 4, :],
                   V(pb2.h[:, 0:4 * NSQ].rearrange("p (j s) -> p j s", j=4), [pb2.base]))
        for wn, (sc0, nm) in enumerate(((8, "norm1"), (32, "norm2"))):
            for j in range(8):
                S.ts("dve", amod[:, l, wn, j, :], modT[:, l, sc0 + j, :], 1.0, pcol(l, nm, j), op0=ALU.add, op1=ALU.mult)

    def mod_shift(l, wn, j, s):
        return modT[:, l, (0 if wn == 0 else 24) + j, s:s + 1]

    def mod_gate(l, wn, j, s):
        return modT[:, l, (16 if wn == 0 else 40) + j, s:s + 1]

    def halo_load(dst_tile, src2d, C, h):
        AR.reset()
        stg_ = AR.get([16, 5632], F32)
        S.dma_in("sp", stg_[0:h, 0:C * 128], src2d)
        pb = bank()
        for c in range(C):
            S.tr(pb[:, c * h:(c + 1) * h], stg_[0:h, c * 128:(c + 1) * 128], ident[0:h, 0:h])
        S.copy("act", dst_tile[:, :, :], V(pb.h[:, 0:C * h].rearrange("p (c t) -> p c t", c=C), [pb.base]))

    def halo_store(dst2d, src_tile, C, h):
        AR.reset()
        stg_ = AR.get([16, 5632], F32)
        for c0 in range(0, C, 4):
            cn = min(4, C - c0)
            pb = bank()
            for cc in range(cn):
                S.tr(pb[0:h, cc * 128:(cc + 1) * 128], src_tile[:, c0 + cc, :], ident)
            S.copy("act", stg_[0:h, c0 * 128:(c0 + cn) * 128], pb[0:h, 0:cn * 128])
        S.dma_out("pool", dst2d, stg_[0:h, 0:C * 128])

    class SeqState:
        pass

    def new_state(name):
        st = SeqState()
        st.hs = [S.tile([128, SH * SP], F32, "%s_hs%d" % (name, l)) for l in range(L)]
        st.pool_h = [S.tile([128, 8, 15], F32, "%s_ph%d" % (name, l)) for l in range(L)]
        st.sconv_h = [S.tile([128, 24, 3], F32, "%s_sh%d" % (name, l)) for l in range(L)]
        st.fconv_h = [S.tile([128, 44, 2], F32, "%s_fh%d" % (name, l)) for l in range(L)]
        return st

    pstate = new_state("p")
    for l in range(L):
        S.memset("pool", pstate.hs[l][:], 0.0)
        S.memset("pool", pstate.pool_h[l][:], 0.0)
        S.memset("pool", pstate.sconv_h[l][:], 0.0)
        S.memset("pool", pstate.fconv_h[l][:], 0.0)
    sstates = []
    hs_shared = S.tile([128, SH * SP], F32, "s_hs")
    for s in range(NS):
        st = SeqState()
        st.hs = [hs_shared for l in range(L)]
        st.pool_h = [S.tile([128, 8, 15], F32, "s%d_ph%d" % (s, l)) for l in range(L)]
        st.sconv_h = [S.tile([128, 24, 3], F32, "s%d_sh%d" % (s, l)) for l in range(L)]
        st.fconv_h = [S.tile([128, 44, 2], F32, "s%d_fh%d" % (s, l)) for l in range(L)]
        sstates.append(st)
        for l in range(L):
            halo_load(st.pool_h[l], st_pool[l, s], 8, 15)
            halo_load(st.sconv_h[l], st_sconv[l, s], 24, 3)
            halo_load(st.fconv_h[l], st_fconv[l, s], 44, 2)

    xT = S.tile([128, 8, NT], F32, "xT")
    hT = S.tile([128, 8, NT], BF16, "hT")
    rstd = S.tile([128, NT], F32, "rstd")
    tmpn = S.tile([128, NT], F32, "tmpn")
    xtok = S.tile([128, D], F32, "xtok")
    qT = S.tile([128, NH, NT], BF16, "qT")
    qiT = S.tile([128, IH, NT], BF16, "qiT")
    oattnT = S.tile([64, NH, NT], BF16, "oattnT")
    opoolT = S.tile([128, 8, NT], BF16, "opoolT")
    ossdT = S.tile([128, 16, NT], BF16, "ossdT")
    NSEGMAX = max(NT // 128, NS)
    wsc = S.tile([128, NSEGMAX, 8], F32, "wsc")
    dtall = S.tile([128, NSEGMAX, 32], F32, "dtall")
    ropet = S.tile([128, 64], F32, "ropet")
    ssq = S.tile([128, 16], F32, "ssq")
    rq = S.tile([128, 16], F32, "rq")
    bis = S.tile([128, 8], F32, "bis")
    steps = S.tile([128, NBIS + 2], F32, "steps")
    maskT = S.tile([128, NKMAX // 128 + 1, 128 if T >= 128 else TS], BF16, "maskT") if False else \
        S.tile([128, max((T // 128) * 128, (NKS // 128) * TS)], BF16, "maskT")
    ktile = [S.tile([128, 4, 128], BF16, "ktile%d" % i) for i in range(3)]
    vtile = [S.tile([128, 4, 65], BF16, "vtile%d" % i) for i in range(3)]
    pexp = [S.tile([128, 512], BF16, "pexp%d" % i) for i in range(2)]
    pmsk = [S.tile([128, 512], BF16, "pmsk%d" % i) for i in range(2)]
    rden = S.tile([128, 512], F32, "rden")
    osb = S.tile([64, 512], F32, "osb")
    relu_b = [S.tile([128, 512], F32, "relu%d" % i) for i in range(2)]
    ones64 = S.tile([128, 64], F32, "ones64")
    S.memset("dve", ones64[:], 1.0)

    def rmsnorm_mod(l, wn, n, msegs):
        AR.reset()
        sq = AR.get([128, 8, NT], F32)
        for j in range(8):
            S.act(sq[:, j, 0:n], xT[:, j, 0:n], AF.Square)
        pb = bank()
        for j in range(8):
            S.mm(pb[:, 0:n], ones, sq[:, j, 0:n], start=(j == 0), stop=(j == 7))
        S.act(rstd[:, 0:n], pb[:, 0:n], AF.Sqrt, bias=cst_eps[:, 0:1], scale=1.0 / D)
        S.recip(rstd[:, 0:n], rstd[:, 0:n])
        for j in range(8):
            S.tt("dve", tmpn[:, 0:n], xT[:, j, 0:n], rstd[:, 0:n], ALU.mult)
            for (c0, sn, si) in msegs:
                S.act(hT[:, j, c0:c0 + sn], tmpn[:, c0:c0 + sn], AF.Identity,
                      bias=mod_shift(l, wn, j, si), scale=amod[:, l, wn, j, si:si + 1])

    def rope_norm(l, src, c0, nh, n, gain_off, bufs):
        w1, w2, w3 = bufs
        x3 = V(src.h[0:n, c0:c0 + nh * 64].rearrange("p (h d) -> p h d", d=64), [src.base])
        work = V(w1.h[0:n, 0:nh * 64].rearrange("p (h d) -> p h d", d=64), [w1.base])
        work2 = V(w2.h[0:n, 0:nh * 64].rearrange("p (h d) -> p h d", d=64), [w2.base])
        tA = V(w3.h[0:n, 0:nh * 32].rearrange("p (h d) -> p h d", d=32), [w3.base])
        if gain_off is not None:
            S.tt("dve", work, x3, x3, ALU.mult)
            S.reduce(ssq[0:n, 0:nh], work, ALU.add)
            S.act(rq[0:n, 0:nh], ssq[0:n, 0:nh], AF.Sqrt, bias=cst_eps[0:n, 0:1], scale=1.0 / 64)
            S.recip(rq[0:n, 0:nh], rq[0:n, 0:nh])
            rqb = V(rq.h[0:n, 0:nh].rearrange("p (h o) -> p h o", o=1).to_broadcast([n, nh, 64]), [rq.base])
            S.tt("dve", work, x3, rqb, ALU.mult)
            gb = V(frw.h[0:n, l, gain_off:gain_off + 64].rearrange("p (o d) -> p o d", o=1).to_broadcast([n, nh, 64]), [frw.base])
            S.tt("dve", work, work, gb, ALU.mult)
            xin = work
        else:
            xin = x3
        cosb = V(ropet.h[0:n, 0:32].rearrange("p (o d) -> p o d", o=1).to_broadcast([n, nh, 32]), [ropet.base])
        sinb = V(ropet.h[0:n, 32:64].rearrange("p (o d) -> p o d", o=1).to_broadcast([n, nh, 32]), [ropet.base])
        x1 = V(xin.ap[:, :, 0:32], xin.trks)
        x2 = V(xin.ap[:, :, 32:64], xin.trks)
        o1 = V(work2.ap[:, :, 0:32], work2.trks)
        o2 = V(work2.ap[:, :, 32:64], work2.trks)
        S.tt("dve", o1, x1, cosb, ALU.mult)
        S.tt("pool", tA, x2, sinb, ALU.mult)
        S.tt("dve", o1, o1, tA, ALU.subtract)
        S.tt("dve", o2, x1, sinb, ALU.mult)
        S.tt("pool", tA, x2, cosb, ALU.mult)
        S.tt("dve", o2, o2, tA, ALU.add)
        return V(w2.h[0:n, 0:nh * 64], [w2.base])

    def stage_B(l, segs):
        hl = 64 * l
        AR.reset()
        stg = [AR.get([128, 640], F32) for _ in range(2)]
        stb = [AR.get([128, 640], BF16) for _ in range(2)]
        w1 = AR.get([128, 512], F32)
        w2 = AR.get([128, 512], F32)
        w3 = AR.get([128, 256], F32)
        bufs = (w1, w2, w3)
        vb = AR.get([128, 4, 65], BF16)
        blocks = (("q0", O_Q, 512), ("q1", O_Q + 512, 512), ("kv", O_K, 512), ("qi", O_QI, 512), ("kiw", O_KI, 72), ("dt", O_DT, 32))
        si_ = 0
        for (bn, c0w, ncol) in blocks:
            wv = wload(wsrc("w_in", l, 0, 8, c0w, ncol), 8, ncol)
            for sg in segs:
                n = sg.n
                cs0 = sg.c0
                sgt = stg[si_ % 2]
                sbt = stb[si_ % 2]
                si_ += 1
                pb = bank()
                for kc in range(8):
                    S.mm(pb[0:n, 0:ncol], hT[:, kc, cs0:cs0 + n], wv[:, kc, :], start=(kc == 0), stop=(kc == 7))
                S.copy("act", sgt[0:n, 64:64 + ncol], pb[0:n, 0:ncol])
                if bn in ("q0", "q1", "qi", "kv", "kiw") and True:
                    if bn != "dt":
                        pass
                if bn == "q0" or bn == "q1" or bn == "qi" or bn == "kv" or bn == "kiw":
                    S.dma_in("sp", ropet[0:n, :], sg.rope)
                if bn in ("q0", "q1"):
                    res = rope_norm(l, sgt, 64, 8, n, 0, bufs)
                    S.copy("act", sbt[0:n, 64:64 + 512], res)
                    h0 = 0 if bn == "q0" else 8
                    for hb in range(2):
                        pbt = bank()
                        pbv = bfv(pbt)
                        for hh in range(4):
                            h = hb * 4 + hh
                            S.tr(pbv[:, hh * n:(hh + 1) * n], sbt[0:n, 64 + 64 * h - hl:64 + 64 * h - hl + 128], identb[0:n, 0:n])
                        S.copy("dve", V(qT.h[hl:hl + 64, h0 + hb * 4:h0 + hb * 4 + 4, cs0:cs0 + n], [qT.base]),
                               V(pbv.ap0[hl:hl + 64, 0:4 * n].rearrange("p (h t) -> p h t", h=4), [pbt.base]))
                elif bn == "qi":
                    res = rope_norm(l, sgt, 64, 8, n, None, bufs)
                    S.copy("act", sbt[0:n, 64:64 + 512], res)
                    for hb in range(2):
                        pbt = bank()
                        pbv = bfv(pbt)
                        for hh in range(4):
                            h = hb * 4 + hh
                            S.tr(pbv[:, hh * n:(hh + 1) * n], sbt[0:n, 64 + 64 * h - hl:64 + 64 * h - hl + 128], identb[0:n, 0:n])
                        S.copy("dve", V(qiT.h[hl:hl + 64, hb * 4:hb * 4 + 4, cs0:cs0 + n], [qiT.base]),
                               V(pbv.ap0[hl:hl + 64, 0:4 * n].rearrange("p (h t) -> p h t", h=4), [pbt.base]))
                elif bn == "kv":
                    res = rope_norm(l, sgt, 64, 4, n, 64, bufs)
                    S.dma_out("pool", sg.nk_out(l), res)
                    S.dma_out("pool", sg.nv_out(l), sgt[0:n, 64 + 256:64 + 512])
                    S.copy("act", sbt[0:n, 64:64 + 256], res)
                    pbt = bank()
                    pbv = bfv(pbt)
                    for g in range(4):
                        S.tr(pbv[:, g * n:(g + 1) * n], sbt[0:n, 64 + 64 * g:64 + 64 * g + 128], identb[0:n, 0:n])
                    ktn = ktile[0]
                    S.copy("dve", V(ktn.h[0:64, :, 0:n], [ktn.base]),
                           V(pbv.ap0[0:64, 0:4 * n].rearrange("p (h t) -> p h t", h=4), [pbt.base]))
                    S.dma_out("pool", V(sg.kt.ap[:, :, sg.key0:sg.key0 + n], sg.kt.trks), V(ktn.h[0:64, :, 0:n], [ktn.base]))
                    S.memset("pool", vb[0:n, :, 64:65], 1.0)
                    S.copy("dve", vb[0:n, :, 0:64], V(sgt.h[0:n, 64 + 256:64 + 512].rearrange("p (g d) -> p g d", g=4), [sgt.base]))
                    S.dma_out("pool", V(sg.va.ap[sg.key0:sg.key0 + n, :, :], sg.va.trks), vb[0:n, :, :])
                elif bn == "kiw":
                    res = rope_norm(l, sgt, 64, 1, n, None, bufs)
                    S.dma_out("pool", sg.nki_out(l), res)
                    S.copy("act", sbt[0:n, 64:64 + 64], res)
                    pbt = bank()
                    pbv = bfv(pbt)
                    S.tr(pbv[:, 0:n], sbt[0:n, 64:64 + 128], identb[0:n, 0:n])
                    ktn = ktile[1]
                    S.copy("dve", V(ktn.h[0:64, 0, 0:n], [ktn.base]), pbv[0:64, 0:n])
                    S.dma_out("pool", V(sg.kit.ap[:, sg.key0:sg.key0 + n], sg.kit.trks), V(ktn.h[0:64, 0, 0:n], [ktn.base]))
                    S.ts("dve", wsc[0:n, sg.i, :], sgt[0:n, 64 + 64:64 + 72], float(IH) ** -0.5)
                elif bn == "dt":
                    S.tt("dve", sgt[0:n, 64:96], sgt[0:n, 64:96], frw[0:n, l, 128:160], ALU.add)
                    S.act(sgt[0:n, 64:96], sgt[0:n, 64:96], AF.Exp)
                    S.act(dtall[0:n, sg.i, :], sgt[0:n, 64:96], AF.Ln, bias=ones64[0:n, 0:1])

    def stage_C(l, sg):
        hl = 64 * l
        n = sg.n
        nkeys = sg.key0 + n
        topk = sg.topk
        nkt = (nkeys + 127) // 128
        AR.reset()
        isc = AR.get([128, NKMAX], F32)
        imk = AR.get([128, NKMAX], BF16)
        kitb = imk
        S.dma_in("sp", V(kitb.h[hl:hl + 64, 0:nkeys], [kitb.base]), V(sg.kit.ap[:, 0:nkeys], sg.kit.trks))
        ri = 0
        for k0 in range(0, nkeys, 512):
            kn = min(512, nkeys - k0)
            for h in range(IH):
                pb = bank()
                S.mm(pb[0:n, 0:kn], V(qiT.h[hl:hl + 64, h, sg.c0:sg.c0 + n], [qiT.base]),
                     V(kitb.h[hl:hl + 64, k0:k0 + kn], [kitb.base]))
                rb = relu_b[ri % 2]
                ri += 1
                S.act(rb[0:n, 0:kn], pb[0:n, 0:kn], AF.Relu, scale=0.125)
                if h == 0:
                    S.ts("dve", isc[0:n, k0:k0 + kn], rb[0:n, 0:kn], wsc[0:n, sg.i, 0:1])
                else:
                    S.stt(isc[0:n, k0:k0 + kn], rb[0:n, 0:kn], wsc[0:n, sg.i, h:h + 1], isc[0:n, k0:k0 + kn], ALU.mult, ALU.add)
        if nkeys > topk:
            S.reduce(bis[0:n, 0:1], isc[0:n, 0:nkeys], ALU.max)
            S.reduce(bis[0:n, 1:2], isc[0:n, 0:nkeys], ALU.min)
        S.tt("dve", isc[0:n, sg.key0:sg.key0 + n], isc[0:n, sg.key0:sg.key0 + n], negqk[0:n, 0:n], ALU.add)
        if nkeys <= topk:
            S.ts("dve", imk[0:n, 0:nkeys], isc[0:n, 0:nkeys], -1.0e29, None, op0=ALU.is_ge)
        else:
            S.ts("dve", bis[0:n, 0:1], bis[0:n, 0:1], 1.0e-3, None, op0=ALU.add)
            S.tt("dve", bis[0:n, 2:3], bis[0:n, 0:1], bis[0:n, 1:2], ALU.subtract)
            S.ts("dve", steps[0:n, :], pw2[0:n, :], bis[0:n, 2:3])
            S.tt("dve", bis[0:n, 3:4], bis[0:n, 1:2], steps[0:n, 0:1], ALU.add)
            for k in range(NBIS):
                S.ts("dve", imk[0:n, 0:nkeys], isc[0:n, 0:nkeys], bis[0:n, 3:4], 0.0, op0=ALU.is_ge, op1=ALU.add,
                     accum=bis[0:n, 4:5])
                S.ts("dve", bis[0:n, 5:6], bis[0:n, 4:5], topk - 0.5, 0.5, op0=ALU.is_ge, op1=ALU.subtract)
                S.stt(bis[0:n, 3:4], bis[0:n, 5:6], steps[0:n, k:k + 1], bis[0:n, 3:4], ALU.mult, ALU.add)
            S.tt("dve", bis[0:n, 6:7], bis[0:n, 3:4], steps[0:n, NBIS:NBIS + 1], ALU.subtract)
            S.ts("dve", bis[0:n, 6:7], bis[0:n, 6:7], -1.0e29, None, op0=ALU.max)
            S.ts("dve", imk[0:n, 0:nkeys], isc[0:n, 0:nkeys], bis[0:n, 6:7], None, op0=ALU.is_ge)
        for j0 in range(0, nkt, 4):
            pbt = bank()
            pbv = bfv(pbt)
            jn = min(4, nkt - j0)
            nks = []
            for jj in range(jn):
                j = j0 + jj
                nk = min(128, nkeys - j * 128)
                nks.append(nk)
                S.tr(pbv[0:nk, jj * n:(jj + 1) * n], imk[0:n, j * 128:j * 128 + nk], identb[0:n, 0:n])
            if all(k_ == 128 for k_ in nks):
                S.copy("act", maskT[:, j0 * n:(j0 + jn) * n], pbv[:, 0:jn * n])
            else:
                for jj in range(jn):
                    S.copy("act", maskT[0:nks[jj], (j0 + jj) * n:(j0 + jj + 1) * n], pbv[0:nks[jj], jj * n:(jj + 1) * n])
        for j in range(nkt):
            nk = min(128, nkeys - j * 128)
            kt_ = ktile[j % 3]
            vt_ = vtile[j % 3]
            S.dma_in("sp", V(kt_.h[hl:hl + 64, :, 0:nk], [kt_.base]), V(sg.kt.ap[:, :, j * 128:j * 128 + nk], sg.kt.trks))
            S.dma_in("sp", vt_[0:nk, :, :], V(sg.va.ap[j * 128:j * 128 + nk, :, :], sg.va.trks))
            for g in range(4):
                pb = bank()
                S.mm(pb[0:nk, 0:4 * n], V(kt_.h[hl:hl + 64, g, 0:nk], [kt_.base]),
                     V(qT.h[hl:hl + 64, 4 * g:4 * g + 4, sg.c0:sg.c0 + n], [qT.base]))
                pe_ = pexp[(j * 4 + g) % 2]
                pm_ = pmsk[(j * 4 + g) % 2]
                S.act(pe_[0:nk, 0:4 * n], pb[0:nk, 0:4 * n], AF.Exp, scale=float(HD) ** -0.5)
                mk = V(maskT.h[0:nk, j * n:(j + 1) * n].rearrange("p (o q) -> p o q", o=1).to_broadcast([nk, 4, n]), [maskT.base])
                S.tt("dve", V(pm_.h[0:nk, 0:4 * n].rearrange("p (h q) -> p h q", h=4), [pm_.base]),
                     V(pe_.h[0:nk, 0:4 * n].rearrange("p (h q) -> p h q", h=4), [pe_.base]), mk, ALU.mult)
                S.mm(PS[4 + g][0:65, 0:4 * n], vt_[0:nk, g, :], pm_[0:nk, 0:4 * n], start=(j == 0), stop=(j == nkt - 1))
        for g in range(4):
            acc = PS[4 + g]
            S.recip(rden[64:65, 0:4 * n], acc[64:65, 0:4 * n])
            pb = bank()
            S.mm(pb[0:64, 0:4 * n], ones64[64:65, 0:64], rden[64:65, 0:4 * n])
            S.copy("act", osb[0:64, 0:4 * n], acc[0:64, 0:4 * n])
            S.tt("dve", V(oattnT.h[0:64, 4 * g:4 * g + 4, sg.c0:sg.c0 + n], [oattnT.base]),
                 V(osb.h[0:64, 0:4 * n].rearrange("p (h q) -> p h q", h=4), [osb.base]),
                 V(pb.h[0:64, 0:4 * n].rearrange("p (h q) -> p h q", h=4), [pb.base]), ALU.mult)

    def fm_proj(l, c0w, noc, n, consume, name="w_in", kc=8, blk=512):
        per = blk // 128
        for b0 in range(0, noc, per):
            nb = min(per, noc - b0)
            wv = wload(wsrc(name, l, 0, kc, c0w + b0 * 128, nb * 128), kc, nb * 128)
            for o in range(nb):
                pb = bank()
                for k in range(kc):
                    S.mm(pb[:, 0:n], wv[:, k, o * 128:(o + 1) * 128], hT[:, k, 0:n], start=(k == 0), stop=(k == kc - 1))
                consume(b0 + o, pb)

    def stage_D(l, n, csegs, first_prompt):
        AR.reset()
        nsg = len(csegs)
        sn = csegs[0][1]
        W = 15 + sn
        U = AR.get([128, 8, nsg, W], F32)
        pA = AR.get([128, 2, nsg, W], F32)
        pB = AR.get([128, 2, nsg, W], F32)
        pooled = AR.get([128, 8, n], BF16)
        for si, (c0, sn_, st) in enumerate(csegs):
            S.copy("pool", U[:, :, si, 0:15], st.pool_h[l][:, :, :])

        def cons(oc, pb):
            S.copy("act", U[:, oc, :, 15:15 + sn], V(pb.h[:, 0:n].rearrange("p (s t) -> p s t", s=nsg), [pb.base]))
        fm_proj(l, O_PU, 8, n, cons)
        for si, (c0, sn_, st) in enumerate(csegs):
            S.copy("pool", st.pool_h[l][:, :, :], U[:, :, si, sn:sn + 15])
        pw = wload(V(wbf["pool_w"].ap[l].rearrange("g (c p) d -> p (g c) d", p=128), wbf["pool_w"].trks), 8, 256)
        for gi, wnd in enumerate((2, 4, 8, 16)):
            src = V(U.h[:, 2 * gi:2 * gi + 2, :, :], [U.base])
            lo = 0
            sh = 1
            k = 0
            while sh < wnd:
                dst = pA if k % 2 == 0 else pB
                S.tt("pool", V(dst.h[:, :, :, lo + sh:W], [dst.base]), V(src.ap[:, :, :, lo + sh:W], src.trks),
                     V(src.ap[:, :, :, lo:W - sh], src.trks), ALU.add)
                src = V(dst.h[:, :, :, :], [dst.base])
                lo += sh
                sh *= 2
                k += 1
            if first_prompt:
                fx = V(cfix.h[:, gi, 0:15].rearrange("p (a b t) -> p a b t", a=1, b=1).to_broadcast([128, 2, nsg, 15]), [cfix.base])
                S.tt("pool", V(src.ap[:, :, :, 15:30], src.trks), V(src.ap[:, :, :, 15:30], src.trks), fx, ALU.mult)
            S.stt(V(pooled.h[:, 2 * gi:2 * gi + 2, 0:n].rearrange("p c (s t) -> p c s t", s=nsg), [pooled.base]),
                  V(src.ap[:, :, :, 15:15 + sn], src.trks), 1.0 / wnd,
                  V(U.h[:, 2 * gi:2 * gi + 2, :, 15:15 + sn], [U.base]), ALU.mult, ALU.subtract)
            for dc in range(2):
                pb = bank()
                for cc in range(2):
                    S.mm(pb[:, 0:n], pw[:, gi * 2 + cc, dc * 128:(dc + 1) * 128], pooled[:, 2 * gi + cc, 0:n],
                         start=(cc == 0), stop=(cc == 1))
                S.act(opoolT[:, 2 * gi + dc, 0:n], pb[:, 0:n], AF.Copy, scale=pcol(l, "pool_scale", 2 * gi + dc))

    def conv_fm(dst_f32, pb, n, csegs, halo_tiles, oc, hw, wname, bname, l, stgc):
        nsg = len(csegs)
        sn = csegs[0][1]
        for si, (c0, sn_, st) in enumerate(csegs):
            S.copy("pool", stgc[:, si, 0:hw], halo_tiles(st)[:, oc, :])
        S.copy("act", stgc[:, :, hw:hw + sn], V(pb.h[:, 0:n].rearrange("p (s t) -> p s t", s=nsg), [pb.base]))
        for si, (c0, sn_, st) in enumerate(csegs):
            S.copy("pool", halo_tiles(st)[:, oc, :], stgc[:, si, sn:sn + hw])
        dv = V(dst_f32.ap.rearrange("p (s t) -> p s t", s=nsg), dst_f32.trks)
        ntap = hw + 1
        wo, _ = PCOLS[wname]
        for j in range(ntap):
            wj = ptb[:, l, wo + j * (24 if ntap == 4 else 44) + oc:wo + j * (24 if ntap == 4 else 44) + oc + 1]
            if j == 0:
                S.ts("dve", dv, stgc[:, :, 0:sn], wj, pcol(l, bname, oc), op0=ALU.mult, op1=ALU.add)
            else:
                S.stt(dv, stgc[:, :, j:j + sn], wj, dv, ALU.mult, ALU.add)

    def stage_E(l, n, csegs, segs):
        AR.reset()
        nsg = len(csegs)
        sn = csegs[0][1]
        xbcT = AR.get([128, 24, NT], BF16)
        szT = AR.get([128, 16, NT], BF16)
        stgc = AR.get([128, nsg, 3 + sn], F32)
        cres = AR.get([128, NT], F32)

        def cons_z(oc, pb):
            S.act(szT[:, oc, 0:n], pb[:, 0:n], AF.Silu)
        fm_proj(l, O_Z, 16, n, cons_z)

        def cons_x(oc, pb):
            conv_fm(cres[:, 0:n], pb, n, csegs, lambda st: st.sconv_h[l], oc, 3, "sconv_w", "sconv_b", l, stgc)
            S.act(xbcT[:, oc, 0:n], cres[:, 0:n], AF.Silu)
        fm_proj(l, O_XBC, 24, n, cons_x)

        xdt = AR.get([128, 32, 64], BF16)
        xdtd = AR.get([128, 32, 64], BF16)
        Btok = AR.get([128, 4, 128], BF16)
        atok = AR.get([128, 32], F32)
        cstok = AR.get([128, 32], F32)
        dec = AR.get([128, 32], F32)
        cdec = AR.get([128, 32], F32)
        abc = AR.get([128, 8, 128], F32)
        Eg = AR.get([128, 8, 128], F32)
        Egb = AR.get([128, 8, 128], BF16)
        Mg = AR.get([128, 8, 128], BF16)
        Csg = AR.get([128, 8, 128], BF16)
        CBm = AR.get([128, 128], BF16)
        hbf = AR.get([128, 512], BF16)
        htmp = AR.get([128, 512], F32)
        y3 = AR.get([128, 4, 128], F32)
        ysq = AR.get([128, 4, 128], F32)
        rs = AR.get([128, 128], F32)
        for sg in segs:
            m = sg.n
            c0 = sg.c0
            st = sg.state
            hs = st.hs[l]
            if sg.load_state is not None:
                sg.load_state(l)
            for half in range(2):
                pbt = bank()
                pbv = bfv(pbt)
                for o in range(8):
                    S.tr(pbv[0:m, o * 128:(o + 1) * 128], xbcT[:, half * 8 + o, c0:c0 + m], identb)
                dtb = V(dtall.h[0:m, sg.i, half * 16:half * 16 + 16].rearrange("p (h o) -> p h o", o=1).to_broadcast([m, 16, 64]), [dtall.base])
                S.tt("dve", xdt[0:m, half * 16:half * 16 + 16, :],
                     V(pbv.ap0[0:m, 0:1024].rearrange("p (h d) -> p h d", d=64), [pbt.base]), dtb, ALU.mult)
            pbt = bank()
            pbv = bfv(pbt)
            for g in range(4):
                S.tr(pbv[0:m, g * 128:(g + 1) * 128], xbcT[:, 16 + g, c0:c0 + m], identb)
            S.copy("act", V(Btok.h[0:m, :, :], [Btok.base]), V(pbv.ap0[0:m, 0:512].rearrange("p (g s) -> p g s", g=4), [pbt.base]))
            S.tt("dve", atok[0:m, :], dtall[0:m, sg.i, :], Abc[0:m, l, :], ALU.mult)
            pb = bank()
            S.mm(pb[0:m, 0:32], tri[0:m, 0:m], atok[0:m, :])
            S.mm(pb[:, 32:64], ones[0:m, :], atok[0:m, :])
            S.copy("act", cstok[0:m, :], pb[0:m, 0:32])
            S.tt("dve", dec[0:m, :], pb[0:m, 32:64], cstok[0:m, :], ALU.subtract)
            S.act(dec[0:m, :], dec[0:m, :], AF.Exp)
            S.act(cdec[:, :], pb[:, 32:64], AF.Exp)
            decb = V(dec.h[0:m, :].rearrange("p (h o) -> p h o", o=1).to_broadcast([m, 32, 64]), [dec.base])
            S.tt("dve", xdtd[0:m, :, :], xdt[0:m, :, :], decb, ALU.mult)
            for g in range(4):
                ab = V(atok.h[0:m, 8 * g:8 * g + 8].rearrange("p (h o) -> p h o", o=1).to_broadcast([m, 8, 128]), [atok.base])
                S.copy("pool", abc[0:m, :, :], ab)
                pcs = [bank(), bank()]
                for hh in range(8):
                    S.mm(pcs[hh // 4][:, (hh % 4) * m:(hh % 4 + 1) * m], abc[0:m, hh, :], tri[0:m, 0:m])
                pb = bank()
                S.mm(pb[0:m, 0:m], xbcT[:, 16 + g, c0:c0 + m], xbcT[:, 20 + g, c0:c0 + m])
                S.tt("dve", CBm[0:m, 0:m], pb[0:m, 0:m], tri[0:m, 0:m], ALU.mult)
                for hh in range(8):
                    S.stt(Eg[0:m, hh, 0:m], pcs[hh // 4][0:m, (hh % 4) * m:(hh % 4 + 1) * m], cstok[0:m, 8 * g + hh:8 * g + hh + 1],
                          negsl[0:m, 0:m], ALU.subtract, ALU.min)
                S.act(Egb[0:m, :, 0:m], Eg[0:m, :, 0:m], AF.Exp)
                cbb = V(CBm.h[0:m, 0:m].rearrange("p (o q) -> p o q", o=1).to_broadcast([m, 8, m]), [CBm.base])
                S.tt("dve", Mg[0:m, :, 0:m], Egb[0:m, :, 0:m], cbb, ALU.mult)
                for q in range(2):
                    S.act(Eg[:, q * 4:q * 4 + 4, 0:m], V(pcs[q].h[:, 0:4 * m].rearrange("p (h t) -> p h t", h=4), [pcs[q].base]), AF.Exp)
                ctb = V(xbcT.h[:, 20 + g, c0:c0 + m].rearrange("p (o t) -> p o t", o=1).to_broadcast([128, 8, m]), [xbcT.base])
                S.tt("dve", Csg[:, :, 0:m], Eg[:, :, 0:m], ctb, ALU.mult)
                S.copy("act", hbf[:, :], hs[:, g * 512:(g + 1) * 512])
                yb = PS[4 + (g % 2)]
                for hh in range(8):
                    h = 8 * g + hh
                    po = 64 * (h % 2)
                    jj = hh // 2
                    S.mm(yb[po:po + 64, jj * m:(jj + 1) * m], xdt[0:m, h, :], Mg[0:m, hh, 0:m], start=True, stop=False)
                    S.mm(yb[po:po + 64, jj * m:(jj + 1) * m], hbf[:, hh * 64:(hh + 1) * 64], Csg[:, hh, 0:m], start=False, stop=True)
                pbs = bank()
                S.mm(pbs[:, 0:512], Btok[0:m, g, :], V(xdtd.h[0:m, 8 * g:8 * g + 8, :].rearrange("p h d -> p (h d)"), [xdtd.base]))
                cdb = V(cdec.h[:, 8 * g:8 * g + 8].rearrange("p (h o) -> p h o", o=1).to_broadcast([128, 8, 64]), [cdec.base])
                S.tt("dve", V(htmp.h[:, :].rearrange("p (h d) -> p h d", d=64), [htmp.base]),
                     V(hs.h[:, g * 512:(g + 1) * 512].rearrange("p (h d) -> p h d", d=64), [hs.base]), cdb, ALU.mult)
                S.tt("dve", hs[:, g * 512:(g + 1) * 512], htmp[:, :], pbs[:, 0:512], ALU.add)
                for jj in range(4):
                    oc = 4 * g + jj
                    S.stt(y3[:, jj, 0:m], xbcT[:, oc, c0:c0 + m], pcol(l, "d_col", oc), yb[:, jj * m:(jj + 1) * m], ALU.mult, ALU.add)
                S.tt("dve", y3[:, :, 0:m], y3[:, :, 0:m], szT[:, 4 * g:4 * g + 4, c0:c0 + m], ALU.mult)
                S.act(ysq[:, :, 0:m], y3[:, :, 0:m], AF.Square)
                pbn = bank()
                for jj in range(4):
                    S.mm(pbn[:, 0:m], ones, ysq[:, jj, 0:m], start=(jj == 0), stop=(jj == 3))
                S.act(rs[:, 0:m], pbn[:, 0:m], AF.Sqrt, bias=cst_eps[:, 0:1], scale=1.0 / 512)
                S.recip(rs[:, 0:m], rs[:, 0:m])
                for jj in range(4):
                    oc = 4 * g + jj
                    S.stt(ossdT[:, oc, c0:c0 + m], y3[:, jj, 0:m], pcol(l, "ssd_norm", oc), rs[:, 0:m], ALU.mult, ALU.mult)
            if sg.store_state is not None:
                sg.store_state(l)

    def stage_F(l, n, msegs):
        AR.reset()
        macc = AR.get([128, 8, NT], F32)
        mergedT = AR.get([128, 8, NT], BF16)
        sig = [AR.get([128, NT], F32) for _ in range(2)]
        branches = (("wb_attn", 16, 64, lambda k: V(oattnT.h[0:64, k, 0:n], [oattnT.base])),
                    ("wb_pool", 8, 128, lambda k: opoolT[:, k, 0:n]),
                    ("wb_ssd", 16, 128, lambda k: ossdT[:, k, 0:n]))
        for b, (wn, kc, rows, rhs_fn) in enumerate(branches):
            for dh in range(4):
                gw = wload(wsrc("w_in", l, 0, 8, O_G + b * 1024 + dh * 256, 256), 8, 256)
                bw = wload(wsrc(wn, l, 0, kc, dh * 256, 256, rows=rows), kc, 256, rows=rows)
                for d2 in range(2):
                    dc = dh * 2 + d2
                    pg = bank()
                    for k in range(8):
                        S.mm(pg[:, 0:n], gw[:, k, d2 * 128:(d2 + 1) * 128], hT[:, k, 0:n], start=(k == 0), stop=(k == 7))
                    pbr = bank()
                    for k in range(kc):
                        S.mm(pbr[:, 0:n], bw[:, k, d2 * 128:(d2 + 1) * 128], rhs_fn(k), start=(k == 0), stop=(k == kc - 1))
                    sg_ = sig[dc % 2]
                    S.act(sg_[:, 0:n], pg[:, 0:n], AF.Sigmoid)
                    if b == 0:
                        S.tt("dve", macc[:, dc, 0:n], sg_[:, 0:n], pbr[:, 0:n], ALU.mult)
                    else:
                        S.tt("dve", sg_[:, 0:n], sg_[:, 0:n], pbr[:, 0:n], ALU.mult)
                        if b == 1:
                            S.tt("pool", macc[:, dc, 0:n], macc[:, dc, 0:n], sg_[:, 0:n], ALU.add)
                        else:
                            S.tt("dve", mergedT[:, dc, 0:n], macc[:, dc, 0:n], sg_[:, 0:n], ALU.add)
        for dh in range(2):
            ow = wload(wsrc("w_out", l, 0, 8, dh * 512, 512), 8, 512)
            for d4 in range(4):
                dc = dh * 4 + d4
                pb = bank()
                for k in range(8):
                    S.mm(pb[:, 0:n], ow[:, k, d4 * 128:(d4 + 1) * 128], mergedT[:, k, 0:n], start=(k == 0), stop=(k == 7))
                for (c0, sn, si) in msegs:
                    S.stt(xT[:, dc, c0:c0 + sn], pb[:, c0:c0 + sn], mod_gate(l, 0, dc, si), xT[:, dc, c0:c0 + sn], ALU.mult, ALU.add)

    def stage_G(l, n, msegs, csegs):
        rmsnorm_mod(l, 1, n, msegs)
        AR.reset()
        nsg = len(csegs)
        sn = csegs[0][1]
        actT = AR.get([128, 22, NT], BF16)
        stgc = AR.get([128, nsg, 2 + sn], F32)
        ca = [AR.get([128, NT], F32) for _ in range(2)]
        cg = AR.get([128, NT], F32)
        for j2 in range(11):
            aw = wload(wsrc("ffn_up", l, 0, 8, j2 * 256, 256), 8, 256)
            gwt = wload(wsrc("ffn_up", l, 0, 8, DFF + j2 * 256, 256), 8, 256)
            for o in range(2):
                j = j2 * 2 + o
                pa = bank()
                for k in range(8):
                    S.mm(pa[:, 0:n], aw[:, k, o * 128:(o + 1) * 128], hT[:, k, 0:n], start=(k == 0), stop=(k == 7))
                ca_ = ca[j % 2]
                conv_fm(ca_[:, 0:n], pa, n, csegs, lambda st: st.fconv_h[l], j, 2, "fconv_w", "fconv_b", l, stgc)
                pg = bank()
                for k in range(8):
                    S.mm(pg[:, 0:n], gwt[:, k, o * 128:(o + 1) * 128], hT[:, k, 0:n], start=(k == 0), stop=(k == 7))
                conv_fm(cg[:, 0:n], pg, n, csegs, lambda st: st.fconv_h[l], 22 + j, 2, "fconv_w", "fconv_b", l, stgc)
                S.act(cg[:, 0:n], cg[:, 0:n], AF.Silu)
                S.tt("dve", actT[:, j, 0:n], cg[:, 0:n], ca_[:, 0:n], ALU.mult)
        for dh in range(8):
            dw = wload(wsrc("ffn_down", l, 0, 22, dh * 128, 128), 22, 128)
            pb = bank()
            for k in range(22):
                S.mm(pb[:, 0:n], dw[:, k, :], actT[:, k, 0:n], start=(k == 0), stop=(k == 21))
            for (c0, sn_, si) in msegs:
                S.stt(xT[:, dh, c0:c0 + sn_], pb[:, c0:c0 + sn_], mod_gate(l, 1, dh, si), xT[:, dh, c0:c0 + sn_], ALU.mult, ALU.add)

    def load_x(src_rows, n, c0):
        S.dma_in("sp", xtok[0:n, :], src_rows)
        for hb in range(2):
            pb = bank()
            for jj in range(4):
                j = hb * 4 + jj
                S.tr(pb[:, jj * n:(jj + 1) * n], xtok[0:n, j * 128:(j + 1) * 128], ident[0:n, 0:n])
            S.copy("act", xT[:, hb * 4:hb * 4 + 4, c0:c0 + n], V(pb.h[:, 0:4 * n].rearrange("p (j t) -> p j t", j=4), [pb.base]))

    def store_x(dst_rows, n, c0):
        for hb in range(2):
            pb = bank()
            for jj in range(4):
                j = hb * 4 + jj
                S.tr(pb[0:n, jj * 128:(jj + 1) * 128], xT[:, j, c0:c0 + n], ident)
            S.copy("act", xtok[0:n, hb * 512:(hb + 1) * 512], pb[0:n, 0:512])
        S.dma_out("pool", dst_rows, xtok[0:n, :])

    def store_hs(hs, dst):
        AR.reset()
        hst = AR.get([128, 16, 128], F32)
        for q in range(4):
            pb = bank()
            for jj in range(4):
                j = q * 4 + jj
                S.tr(pb[:, jj * 128:(jj + 1) * 128], hs[:, j * 128:(j + 1) * 128], ident)
            S.copy("act", hst[:, q * 4:q * 4 + 4, :], V(pb.h[:, :].rearrange("p (j t) -> p j t", j=4), [pb.base]))
        S.dma_out("pool", dst.rearrange("(c p) n -> p c n", p=128), hst[:, :, :])

    def store_halos(st, l, dpool, dsconv, dfconv):
        halo_store(dpool, st.pool_h[l], 8, 15)
        halo_store(dsconv, st.sconv_h[l], 24, 3)
        halo_store(dfconv, st.fconv_h[l], 44, 2)

    for ch in range(NCH):
        segs = []
        for i in range(NT // 128):
            sg = Seg()
            sg.i, sg.n, sg.c0 = i, 128, i * 128
            sg.key0 = ch * NT + i * 128
            sg.topk = cfg.topk_p
            sg.state = pstate
            sg.rope = rope_p[sg.key0:sg.key0 + 128, :]
            sg.nk_out = (lambda l, k0=sg.key0: nk_p[l, k0:k0 + 128, :])
            sg.nv_out = (lambda l, k0=sg.key0: nv_p[l, k0:k0 + 128, :])
            sg.nki_out = (lambda l, k0=sg.key0: nki_p[l, k0:k0 + 128, :])
            sg.load_state = None
            sg.store_state = None
            segs.append(sg)
            load_x(xp[sg.key0:sg.key0 + 128, :], 128, sg.c0)
        msegs = [(0, NT, 0)]
        csegs = [(0, NT, pstate)]
        for l in range(L):
            for sg in segs:
                sg.kt, sg.kit, sg.va = pkt[l], pkit[l], pva[l]
            rmsnorm_mod(l, 0, NT, msegs)
            stage_B(l, segs)
            for sg in segs:
                stage_C(l, sg)
            stage_D(l, NT, csegs, ch == 0)
            stage_E(l, NT, csegs, segs)
            stage_F(l, NT, msegs)
            stage_G(l, NT, msegs, csegs)
        for sg in segs:
            store_x(y_p[sg.key0:sg.key0 + 128, :], 128, sg.c0)
    for l in range(L):
        store_hs(pstate.hs[l], nssd_p[l])
        store_halos(pstate, l, npool_p[l], nsconv_p[l], nfconv_p[l])

    NSC = NST
    segs = []
    for s in range(NS):
        sg = Seg()
        sg.i, sg.n, sg.c0 = s, TS, s * TS
        sg.key0 = PAST
        sg.topk = cfg.topk_s
        sg.state = sstates[s]
        sg.rope = rope_s[:, :]
        sg.nk_out = (lambda l, s=s: nk_s[l, s * TS:(s + 1) * TS, :])
        sg.nv_out = (lambda l, s=s: nv_s[l, s * TS:(s + 1) * TS, :])
        sg.nki_out = (lambda l, s=s: nki_s[l, s * TS:(s + 1) * TS, :])

        def _load(l, s=s):
            AR_ = AR
            for q in range(4):
                S.dma_in("sp", hstg[:, :, :], st_ssd[l, s, q * 512:(q + 1) * 512, :].rearrange("(c p) n -> p c n", p=128))
                pb = bank()
                for jj in range(4):
                    S.tr(pb[:, jj * 128:(jj + 1) * 128], hstg[:, jj, :], ident)
                S.copy("act", hs_shared[:, q * 512:(q + 1) * 512], pb[:, :])

        def _store(l, s=s):
            for q in range(4):
                pb = bank()
                for jj in range(4):
                    j = q * 4 + jj
                    S.tr(pb[:, jj * 128:(jj + 1) * 128], hs_shared[:, j * 128:(j + 1) * 128], ident)
                S.copy("act", hstg[:, :, :], V(pb.h[:, :].rearrange("p (j t) -> p j t", j=4), [pb.base]))
                S.dma_out("pool", nssd_s[l, s, q * 512:(q + 1) * 512, :].rearrange("(c p) n -> p c n", p=128), hstg[:, :, :])
        sg.load_state = _load
        sg.store_state = _store
        segs.append(sg)
    hstg = AliasT(xtok, xtok.h[:, 0:512].rearrange("p (j t) -> p j t", j=4))
    load_x(xs[:, :], NST, 0)
    msegs = [(s * TS, TS, 1 + s) for s in range(NS)]
    csegs = [(s * TS, TS, sstates[s]) for s in range(NS)]
    kpg = AliasT(relu_b[0], relu_b[0].h[:, :].bitcast(BF16)[:, 0:384])
    vpg = AliasT(relu_b[1], relu_b[1].h[:, :].bitcast(BF16)[:, 0:256])
    kipg = AliasT(relu_b[1], relu_b[1].h[:, :].bitcast(BF16)[:, 256:448])
    for l in range(L):
        for s in range(NS):
            for j in range(NPG):
                for h_ in range(PCS):
                    ixh = idxall[:, h_, s * NPG + j:s * NPG + j + 1]
                    S.gather(kpg[:, 64:64 + 256], ck[l * PCS + h_], ixh, PR - 1)
                    S.gather(vpg[:, :], cv[l * PCS + h_], ixh, PR - 1)
                S.gather(kipg[:, 64:128], cki[l], idxall[:, 0, s * NPG + j:s * NPG + j + 1], cfg.NPOOL * 128 - 1)
                pb = bank()
                pbv = bfv(pb)
                for g in range(4):
                    S.tr(pbv[:, g * 128:(g + 1) * 128], kpg[:, 64 + 64 * g:64 + 64 * g + 128], identb)
                kt_ = ktile[j % 3]
                S.copy("act", V(kt_.h[0:64, :, :], [kt_.base]), V(pbv.ap0[0:64, 0:512].rearrange("p (g t) -> p g t", g=4), [pb.base]))
                S.dma_out("sp", V(skt[l][s].ap[:, :, j * 128:(j + 1) * 128], skt[l][s].trks), V(kt_.h[0:64, :, :], [kt_.base]))
                vt_ = vtile[j % 3]
                S.memset("pool", vt_[:, :, 64:65], 1.0)
                S.copy("dve", vt_[:, :, 0:64], V(vpg.h[:, :].rearrange("p (g d) -> p g d", g=4), [vpg.base]))
                S.dma_out("sp", V(sva[l][s].ap[j * 128:(j + 1) * 128, :, :], sva[l][s].trks), vt_[:, :, :])
                pb2 = bank()
                pbv2 = bfv(pb2)
                S.tr(pbv2[:, 0:128], kipg[:, 64:64 + 128], identb)
                ki_ = pexp[j % 2]
                S.copy("act", ki_[0:64, 0:128], pbv2[0:64, 0:128])
                S.dma_out("sp", V(skit[l][s].ap[:, j * 128:(j + 1) * 128], skit[l][s].trks), ki_[0:64, 0:128])
        for s, sg in enumerate(segs):
            sg.kt, sg.kit, sg.va = skt[l][s], skit[l][s], sva[l][s]
        rmsnorm_mod(l, 0, NSC, msegs)
        stage_B(l, segs)
        for sg in segs:
            stage_C(l, sg)
        stage_D(l, NSC, csegs, False)
        stage_E(l, NSC, csegs, segs)
        stage_F(l, NSC, msegs)
        stage_G(l, NSC, msegs, csegs)
        for s in range(NS):
            store_halos(sstates[s], l, npool_s[l, s], nsconv_s[l, s], nfconv_s[l, s])
    store_x(y_s[:, :], NST, 0)

    nops = S.emit()
    return nc, nops


def _consts(cfg):
    c = np.zeros((128, 4, 128), np.float32)
    tri = np.triu(np.ones((128, 128), np.float32))
    c[:, 0, :] = np.eye(128, dtype=np.float32)
    c[:, 1, :] = tri
    c[:, 2, :] = (tri - 1.0) * 1.0e4
    c[:, 3, :] = (tri.T - 1.0) * 1.0e30
    half = 32
    freqs = (np.float32(10000.0) ** (-np.arange(half, dtype=np.float32) / np.float32(half))).astype(np.float32)

    def rope(pos):
        ang = pos.astype(np.float32)[:, None] * freqs[None, :]
        return np.concatenate([np.cos(ang), np.sin(ang)], axis=1).astype(np.float32)
    rp = rope(np.arange(cfg.T))
    rs = rope(cfg.PAST + np.arange(cfg.TS))
    cf = np.ones((128, 4, 16), np.float32)
    for gi, w in enumerate((2, 4, 8, 16)):
        for t in range(15):
            cf[:, gi, t] = float(w) / float(min(w, t + 1))
    return c, rp, rs, cf


def _ptab(inp, l):
    def fm(v):
        return np.ascontiguousarray(v.reshape(-1, 128).T)
    cols = [fm(inp["norm1"][l]), fm(inp["norm2"][l]), fm(inp["pool_scale"][l])]
    cols += [fm(inp["ssd_conv_w"][l][j]) for j in range(4)]
    cols.append(fm(inp["ssd_conv_b"][l]))
    cols.append(fm(inp["ssd_norm"][l]))
    cols += [fm(inp["ffn_conv_w"][l][j]) for j in range(3)]
    cols.append(fm(inp["ffn_conv_b"][l]))
    cols.append(fm(np.repeat(inp["ssd_d"][l], 64)))
    return np.concatenate(cols, axis=1).astype(np.float32)


def run_cfg(cfg, inp, n_cores=8, n_prompt=4):
    nc, nops = build(cfg)
    inp = {k: np.asarray(v) for k, v in inp.items()}
    c, rp, rs, cf = _consts(cfg)
    ptab = np.stack([_ptab(inp, l) for l in range(L)])
    frow = np.stack([np.concatenate([inp["q_norm"][l], inp["k_norm"][l], inp["ssd_dt_bias"][l], inp["ssd_a_log"][l]])[None, :]
                     for l in range(L)]).astype(np.float32)
    NS = cfg.NS
    shared = {
        "b_ada": inp["b_ada"].reshape(L, 1, -1),
        "ptab": ptab, "frow": frow, "consts": c, "rope_p": rp, "rope_s": rs, "cntfix": cf,
    }
    in_maps = []
    for core in range(n_cores):
        b = core % n_prompt
        sl = slice(core * NS, (core + 1) * NS)
        m = dict(shared)
        for nm_, key_, cols_, pcs_ in (("ck_sh", "cache_k", 256, 2), ("cv_sh", "cache_v", 256, 2), ("cki_sh", "cache_kidx", 64, 1)):
            a_ = inp[key_].reshape(L * pcs_, n_cores, -1, cols_)[:, core]
            m[nm_] = a_.reshape(-1, cols_)
        for nm_, key_ in (("w_in", "w_in"), ("pool_w", "pool_w"), ("wb_attn", "w_branch_attn"), ("wb_pool", "w_branch_pool"),
                          ("wb_ssd", "w_branch_ssd"), ("w_out", "w_out"), ("ffn_up", "ffn_up"), ("ffn_down", "ffn_down"),
                          ("w_ada", "w_ada")):
            a_ = inp[key_].reshape(-1, inp[key_].shape[-1])
            rs_ = a_.shape[0] // n_cores
            m[nm_ + "_sh"] = a_[core * rs_:(core + 1) * rs_]
        m["xp"] = inp["x_prompt"][b]
        m["xs"] = inp["x_sample"][sl].reshape(NS * cfg.TS, D)
        m["call"] = np.concatenate([inp["c_prompt"][b:b + 1], inp["c_sample"][sl]], axis=0)
        m["pt"] = inp["page_table"][sl].reshape(1, -1).astype(np.int32)
        m["st_pool"] = inp["state_pool"][:, sl]
        m["st_sconv"] = inp["state_ssd_conv"][:, sl]
        m["st_ssd"] = inp["state_ssd"][:, sl].reshape(L, NS, DSSD, SN)
        m["st_fconv"] = inp["state_ffn_conv"][:, sl]
        in_maps.append({k: np.ascontiguousarray(v) for k, v in m.items()})
    res = run_bass_kernel_spmd(nc, in_maps, core_ids=list(range(n_cores))).results
    T, TS = cfg.T, cfg.TS
    P = range(n_prompt)

    def stackp(name, shp):
        return np.stack([res[b][name].reshape(shp) for b in P], axis=0)

    def stackp_l(name, shp):
        return np.stack([res[b][name].reshape((L,) + shp) for b in P], axis=1)

    def cats_l(name, shp):
        return np.concatenate([res[c_][name].reshape((L, NS) + shp) for c_ in range(n_cores)], axis=1)

    outs = (
        stackp("y_p", (T, D)),
        np.concatenate([res[c_]["y_s"].reshape(NS, TS, D) for c_ in range(n_cores)], axis=0),
        stackp_l("nk_p", (T, NKV, HD)), stackp_l("nv_p", (T, NKV, HD)), stackp_l("nki_p", (T, ID)),
        stackp_l("npool_p", (15, D)), stackp_l("nsconv_p", (3, DCONV)), stackp_l("nssd_p", (SH, SP, SN)),
        stackp_l("nfconv_p", (2, 2 * DFF)),
        cats_l("nk_s", (TS, NKV, HD)), cats_l("nv_s", (TS, NKV, HD)), cats_l("nki_s", (TS, ID)),
        cats_l("npool_s", (15, D)), cats_l("nsconv_s", (3, DCONV)), cats_l("nssd_s", (SH, SP, SN)),
        cats_l("nfconv_s", (2, 2 * DFF)),
    )
    return tuple(np.ascontiguousarray(o, dtype=np.float32) for o in outs)


def kernel(**inputs):
    T = int(np.asarray(inputs["x_prompt"]).shape[1])
    npg = int(np.asarray(inputs["page_table"]).shape[1])
    npool = int(np.asarray(inputs["cache_k"]).shape[1])
    ts = int(np.asarray(inputs["x_sample"]).shape[1])
    cfg = Cfg(T=T, NPG=npg, NPOOL=npool, NS=4, TS=ts, topk_p=min(256, T // 4), topk_s=min(256, (npg * 128 + ts) // 4))
    return run_cfg(cfg, inputs)
```
